# Optimizing a Trainium2 kernel written in Bass

```python
import jax, jax.numpy as jnp
from jax import lax
import numpy as np

D_MODEL = 4096
BATCH = 4
SEQ = 4096
DEPTH = 4

CHUNK = 64
N_META = 16
Q_BLOCK = 128
MLA_HEADS = D_MODEL // 256
NOPE_DIM = 128
ROPE_DIM = 64
QK_DIM = NOPE_DIM + ROPE_DIM
V_DIM = 128
MLA_WIDTH = MLA_HEADS * V_DIM
Q_LORA = D_MODEL // 4
KV_LORA = D_MODEL // 8
POOL_WINDOWS = (2, 4, 8, 16)
POOL_GROUPS = 4
POOL_WIDTH = D_MODEL // 2
POOL_GROUP_DIM = POOL_WIDTH // POOL_GROUPS
ROPE_THETA = 10000.0
NORM_EPS = 1e-6
IN_SPLITS = (Q_LORA, KV_LORA, ROPE_DIM, POOL_WIDTH, MLA_WIDTH, POOL_WIDTH, D_MODEL, D_MODEL)
IN_WIDTH = Q_LORA + KV_LORA + ROPE_DIM + POOL_WIDTH + MLA_WIDTH + POOL_WIDTH + 2 * D_MODEL

kernel_name = "hybrid_mla_pool_gated_merge_trunk"


def _rms(x, g):
    xf = x.astype(jnp.float32)
    y = xf * lax.rsqrt(jnp.mean(xf * xf, axis=-1, keepdims=True) + NORM_EPS)
    return (y * g.astype(jnp.float32)).astype(x.dtype)


def _rope_tables(length):
    pos = jnp.arange(length, dtype=jnp.float32)
    inv = 1.0 / (ROPE_THETA ** (jnp.arange(0, ROPE_DIM, 2, dtype=jnp.float32) / ROPE_DIM))
    ang = pos[:, None] * inv[None, :]
    return jnp.cos(ang)[:, None, :], jnp.sin(ang)[:, None, :]


def _apply_rope(x, cos, sin):
    xf = x.astype(jnp.float32)
    x1, x2 = jnp.split(xf, 2, axis=-1)
    return jnp.concatenate([x1 * cos - x2 * sin, x2 * cos + x1 * sin], axis=-1).astype(x.dtype)


def _attend_block(q_blk, cid_q, k, v, cid_k):
    s = jnp.einsum('bqhd,bkhd->bhqk', q_blk, k, preferred_element_type=jnp.float32) * (QK_DIM ** -0.5)
    mask = cid_k[None, :] <= cid_q[:, None]
    s = jnp.where(mask[None, None], s, jnp.float32(-1e30))
    p = jax.nn.softmax(s, axis=-1).astype(v.dtype)
    return jnp.einsum('bhqk,bkhd->bqhd', p, v)


def _mla(c_q_raw, c_kv_raw, k_rope_raw, q_lora_g, kv_lora_g, w_uq, w_ukv, q_head_g, k_head_g, cos, sin, cid):
    b, length, _ = c_q_raw.shape
    c_q = _rms(c_q_raw, q_lora_g)
    c_kv = _rms(c_kv_raw, kv_lora_g)
    q = (c_q @ w_uq).reshape(b, length, MLA_HEADS, QK_DIM)
    kv = (c_kv @ w_ukv).reshape(b, length, MLA_HEADS, NOPE_DIM + V_DIM)
    k_nope, v = kv[..., :NOPE_DIM], kv[..., NOPE_DIM:]
    k_rope = jnp.broadcast_to(k_rope_raw[:, :, None, :], (b, length, MLA_HEADS, ROPE_DIM))
    k = jnp.concatenate([k_nope, k_rope], axis=-1)
    q = _rms(q, q_head_g)
    k = _rms(k, k_head_g)
    q = jnp.concatenate([q[..., :NOPE_DIM], _apply_rope(q[..., NOPE_DIM:], cos, sin)], axis=-1)
    k = jnp.concatenate([k[..., :NOPE_DIM], _apply_rope(k[..., NOPE_DIM:], cos, sin)], axis=-1)
    out_meta = _attend_block(q[:, :N_META], cid[:N_META], k, v, cid)
    n_blk = (length - N_META) // Q_BLOCK
    q_blocks = jnp.moveaxis(q[:, N_META:].reshape(b, n_blk, Q_BLOCK, MLA_HEADS, QK_DIM), 1, 0)
    cid_blocks = cid[N_META:].reshape(n_blk, Q_BLOCK)
    out_blocks = lax.map(lambda a: _attend_block(a[0], a[1], k, v, cid), (q_blocks, cid_blocks))
    out_real = jnp.moveaxis(out_blocks, 0, 1).reshape(b, length - N_META, MLA_HEADS, V_DIM)
    out = jnp.concatenate([out_meta, out_real], axis=1)
    return out.reshape(b, length, MLA_WIDTH)


def _pool_mixer(u, w_pool, pool_scale):
    b, length, _ = u.shape
    uf = u.astype(jnp.float32)
    cs = jnp.cumsum(uf, axis=1)
    pos1 = jnp.arange(length, dtype=jnp.float32) + 1.0
    pooled = []
    for g, w in enumerate(POOL_WINDOWS):
        c = cs[..., g * POOL_GROUP_DIM:(g + 1) * POOL_GROUP_DIM]
        prev = jnp.pad(c, ((0, 0), (w, 0), (0, 0)))[:, :length]
        count = jnp.minimum(pos1, jnp.float32(w))[None, :, None]
        pooled.append((c - prev) / count)
    pooled = jnp.concatenate(pooled, axis=-1)
    diff = (pooled - uf).astype(u.dtype).reshape(b, length, POOL_GROUPS, POOL_GROUP_DIM)
    mixed = jnp.einsum('blgc,gcd->blgd', diff, w_pool).reshape(b, length, POOL_WIDTH)
    return mixed * pool_scale


def setup_inputs(seed: int = 0) -> dict:
    key = jax.random.key(seed)
    ks = jax.random.split(key, 20)
    f32 = jnp.float32

    def w(k, shape, fan_in):
        return jax.random.normal(k, shape, f32) * (fan_in ** -0.5)

    def gain(k, shape):
        return 1.0 + 0.05 * jax.random.normal(k, shape, f32)

    return {
        "x": jax.random.normal(ks[0], (BATCH, SEQ, D_MODEL), f32),
        "meta_tokens": jax.random.normal(ks[1], (N_META, D_MODEL), f32),
        "norm_g": gain(ks[2], (DEPTH, D_MODEL)),
        "w_in": w(ks[3], (DEPTH, D_MODEL, IN_WIDTH), D_MODEL),
        "q_lora_g": gain(ks[4], (DEPTH, Q_LORA)),
        "kv_lora_g": gain(ks[5], (DEPTH, KV_LORA)),
        "w_uq": w(ks[6], (DEPTH, Q_LORA, MLA_HEADS * QK_DIM), Q_LORA),
        "w_ukv": w(ks[7], (DEPTH, KV_LORA, MLA_HEADS * (NOPE_DIM + V_DIM)), KV_LORA),
        "q_head_g": gain(ks[8], (DEPTH, QK_DIM)),
        "k_head_g": gain(ks[9], (DEPTH, QK_DIM)),
        "w_pool": w(ks[10], (DEPTH, POOL_GROUPS, POOL_GROUP_DIM, POOL_GROUP_DIM), POOL_GROUP_DIM),
        "pool_scale": 1.0 + 0.1 * jax.random.normal(ks[11], (DEPTH, POOL_WIDTH), f32),
        "w_branch_a": w(ks[12], (DEPTH, MLA_WIDTH, D_MODEL), MLA_WIDTH),
        "w_branch_b": w(ks[13], (DEPTH, POOL_WIDTH, D_MODEL), POOL_WIDTH),
        "w_out": w(ks[14], (DEPTH, D_MODEL, D_MODEL), D_MODEL),
    }


def reference(x, meta_tokens, norm_g, w_in, q_lora_g, kv_lora_g, w_uq, w_ukv, q_head_g, k_head_g,
              w_pool, pool_scale, w_branch_a, w_branch_b, w_out):
    b, seq, _ = x.shape
    length = N_META + seq
    h = jnp.concatenate([jnp.broadcast_to(meta_tokens[None].astype(x.dtype), (b, N_META, D_MODEL)), x], axis=1)
    cid = jnp.concatenate([jnp.zeros((N_META,), jnp.int32),
                           jnp.arange(seq, dtype=jnp.int32) // CHUNK + 1])
    cos, sin = _rope_tables(length)
    bounds = []
    acc = 0
    for s in IN_SPLITS[:-1]:
        acc += s
        bounds.append(acc)
    for l in range(DEPTH):
        hn = _rms(h, norm_g[l])
        proj = hn @ w_in[l]
        c_q_raw, c_kv_raw, k_rope_raw, pool_in, gate_a, gate_b, merge_a, merge_b = jnp.split(proj, bounds, axis=-1)
        attn = _mla(c_q_raw, c_kv_raw, k_rope_raw, q_lora_g[l], kv_lora_g[l], w_uq[l], w_ukv[l],
                    q_head_g[l], k_head_g[l], cos, sin, cid)
        br_a = (attn * jax.nn.silu(gate_a)) @ w_branch_a[l]
        pooled = _pool_mixer(pool_in, w_pool[l], pool_scale[l])
        br_b = (pooled * jax.nn.silu(gate_b)) @ w_branch_b[l]
        merged = jax.nn.sigmoid(merge_a) * br_a + jax.nn.sigmoid(merge_b) * br_b
        h = h + merged @ w_out[l]
    return h[:, N_META:]
```

```python
import contextlib
import numpy as np
import concourse.bass as bass
import concourse.mybir as mybir
from concourse.bass_utils import run_bass_kernel_spmd

F32 = mybir.dt.float32
BF16 = mybir.dt.bfloat16
AF = mybir.ActivationFunctionType
ALU = mybir.AluOpType

D = 4096
KC = 32
DEPTH = 4
NT = 8
TS = 512
NM = 16
IN_W = 15936
OFF = dict(cq=0, ckv=1024, kr=1536, pool=1600, ga=3648, gb=5696, ma=7744, mb=11840)
EPS = 1e-6
NW = 5
NUNIT = 8
KEYS = NM + NT * TS
NKT = 1 + 32
WINS = (2, 4, 8, 16)
RB = 33344
SBY = 18432


class Ev:
    __slots__ = ("sem", "val")

    def __init__(self, sem, val):
        self.sem = sem
        self.val = val


class Eng:
    def __init__(self, name):
        self.name = name
        self.ops = []
        self.sem = None
        self.cnt = 0
        self.waited = {}
        self.last = None

    def new_epoch(self, sem):
        self.sem = sem
        self.cnt = 0

    def wait(self, ev):
        if ev is None:
            return
        if isinstance(ev, (list, tuple)):
            for e in ev:
                self.wait(e)
            return
        k = id(ev.sem)
        if self.waited.get(k, 0) >= ev.val:
            return
        self.waited[k] = ev.val
        self.ops.append(("w", ev.sem, ev.val))

    def op(self, fn, sig=True, deps=None):
        self.wait(deps)
        if sig:
            self.cnt += 1
            self.ops.append(("o", fn, self.sem, 1))
            self.last = Ev(self.sem, self.cnt)
            return self.last
        self.ops.append(("o", fn, None, 0))
        return None

    def dma(self, fn, dsem, deps=None):
        self.wait(deps)
        dsem[1] += 16
        self.ops.append(("o", fn, dsem[0], 16))
        return Ev(dsem[0], dsem[1])

    def replay(self, e):
        for o in self.ops:
            if o[0] == "w":
                e.wait_ge(o[1], o[2])
            else:
                ins = o[1](e)
                if o[2] is not None:
                    ins.then_inc(o[2], o[3])


class Unit:
    __slots__ = ("u", "free")

    def __init__(self, u):
        self.u = u
        self.free = None


_DBG = {}


def _deadlock_check(engs):
    sems = {}
    pcs = [0] * len(engs)
    progress = True
    while progress:
        progress = False
        for i, en in enumerate(engs):
            while pcs[i] < len(en.ops):
                o = en.ops[pcs[i]]
                if o[0] == "w":
                    if sems.get(id(o[1]), 0) >= o[2]:
                        pcs[i] += 1
                        progress = True
                    else:
                        break
                else:
                    if o[2] is not None:
                        sems[id(o[2])] = sems.get(id(o[2]), 0) + o[3]
                    pcs[i] += 1
                    progress = True
    stuck = [(en.name, pcs[i], len(en.ops)) for i, en in enumerate(engs) if pcs[i] < len(en.ops)]
    if stuck:
        msg = []
        for i, en in enumerate(engs):
            if pcs[i] < len(en.ops):
                o = en.ops[pcs[i]]
                msg.append("%s pc=%d/%d waits val=%d have=%d" % (en.name, pcs[i], len(en.ops), o[2], sems.get(id(o[1]), 0)))
        raise RuntimeError("DEADLOCK: " + "; ".join(msg))


def build_program(layers, wdepth=DEPTH, ntiles=NT):
    nc = bass.Bass("TRN2", target_bir_lowering=False)

    def din(name, shape):
        return nc.dram_tensor(name, list(shape), F32, kind="ExternalInput").ap()

    xin = din("xin", [NT * TS, D])
    meta_in = din("meta", [NM, D])
    w_in = din("w_in", [wdepth, D, IN_W] if not _DBG.get("tinyw") else [1, 512, 512])
    w_uq = din("w_uq", [wdepth, 1024, 3072] if not _DBG.get("tinyw") else [1, 512, 512])
    w_ukv = din("w_ukv", [wdepth, 512, 4096] if not _DBG.get("tinyw") else [1, 512, 512])
    w_pool = din("w_pool", [wdepth, 4, 512, 512])
    w_ba = din("w_branch_a", [wdepth, 2048, D] if not _DBG.get("tinyw") else [1, 512, 512])
    w_bb = din("w_branch_b", [wdepth, 2048, D] if not _DBG.get("tinyw") else [1, 512, 512])
    w_out = din("w_out", [wdepth, D, D] if not _DBG.get("tinyw") else [1, 512, 512])
    normg_d = din("normg", [128, DEPTH * 32])
    qlg_d = din("qlg", [128, DEPTH * 8])
    kvg_d = din("kvg", [128, DEPTH * 4])
    qhg_d = din("qhg", [128, DEPTH * 4])
    khg_d = din("khg", [128, DEPTH * 4])
    psc_d = din("pscale", [128, DEPTH * 16])
    rope_d = din("ropet", [64, 2, KEYS])
    invc_d = din("invc", [128, 4 * 4 * NM])
    ident_d = din("ident", [128, 128])
    out_d = nc.dram_tensor("out", [NT * TS + NM, D], F32, kind="ExternalOutput").ap()
    hT = [nc.dram_tensor("hT%d" % i, [NT + 1, 128, KC, TS], F32) for i in range(2)]

    def kview(w):
        return w.rearrange("l (kc p) c -> l p kc c", p=128)

    w_in_v, w_uq_v, w_ukv_v = kview(w_in), kview(w_uq), kview(w_ukv)
    w_ba_v, w_bb_v, w_out_v = kview(w_ba), kview(w_bb), kview(w_out)
    w_pool_v = w_pool.rearrange("l g (kc p) c -> l g p kc c", p=128)

    PE, ACT, DVE, POOL, SP = Eng("pe"), Eng("act"), Eng("dve"), Eng("pool"), Eng("sp")

    with contextlib.ExitStack() as es:
        def sb(name, shape, dt):
            return es.enter_context(nc.sbuf_tensor(name, list(shape), dt))

        def sem(name):
            return es.enter_context(nc.semaphore(name))

        def dsem(name):
            return [sem(name), 0]

        hn = sb("hn", [128, KC, TS], BF16)
        cq = sb("cq", [128, 8, TS], BF16)
        Ga = sb("Ga", [128, 16, TS], BF16)
        Gb = sb("Gb", [128, 16, TS], BF16)
        R = sb("R", [128, RB // 2], BF16)
        S = sb("S", [128, SBY // 4], F32)
        cache = sb("cache", [128, 5, KEYS], BF16)
        wbuf = sb("wbuf", [128, NW, 4, 512], BF16)
        ropet = sb("ropet_s", [64, 2, TS], F32)
        normg = sb("normg_s", [128, DEPTH * 32], F32)
        qlg = sb("qlg_s", [128, DEPTH * 8], F32)
        kvg = sb("kvg_s", [128, DEPTH * 4], F32)
        qhg = sb("qhg_s", [128, DEPTH * 4], F32)
        khg = sb("khg_s", [128, DEPTH * 4], F32)
        psc = sb("psc", [128, DEPTH * 16], F32)
        invc = sb("invc_s", [128, 4, 4, NM], F32)
        ident = sb("ident_s", [128, 128], F32)
        ones = sb("ones", [128, 128], BF16)
        epsb = sb("epsb", [128, 1], F32)
        epsk = sb("epsk", [128, 1], F32)
        winv = sb("winv", [128, 4], F32)
        sqs = sb("sqs", [128, 2, TS], BF16)
        Qn = sb("Qn", [128, 2, TS], BF16)
        Qr = sb("Qr", [64, 2, TS], BF16)
        rk = sb("rk", [128, 2, NKT + 1], F32)
        pbuf = sb("pbuf", [128, 3, TS], BF16)
        utail = sb("utail", [128, 16, NM], F32)
        ps = es.enter_context(nc.psum_tensor("ps", [128, 4096], F32))

        Rf = R[:, :].bitcast(F32)
        merged = R[:, 0:KC * TS].rearrange("p (c t) -> p c t", t=TS)
        KVS = RB // 4
        Kh = [R[:, i * KVS:i * KVS + KEYS] for i in range(2)]
        Vh = [R[:, i * KVS + KEYS:i * KVS + KEYS + NKT * 128].rearrange("p (k d) -> p k d", d=128) for i in range(2)]
        rawst = Rf[:, 0:8 * TS].rearrange("p (c t) -> p c t", t=TS)
        UE = TS + NM
        ue = Rf[:, 0:2 * 4 * UE].rearrange("p (a c t) -> p a c t", a=2, c=4)
        uo = Rf[:, 2 * 4 * UE:2 * 4 * UE + 4 * TS].rearrange("p (c t) -> p c t", t=TS)
        dT0 = (2 * 4 * UE + 4 * TS) * 2
        diffT = R[:, dT0:dT0 + 4 * TS].rearrange("p (c t) -> p c t", t=TS)
        hst = S[:, 0:4 * TS].rearrange("p (c t) -> p c t", t=TS)
        sB = S[:, 4 * TS:8 * TS].rearrange("p (c t) -> p c t", t=TS)
        rbc = S[:, 8 * TS:9 * TS]

        wsem = [dsem("w%d" % i) for i in range(NW)]
        hsem = [dsem("h%d" % i) for i in range(4)]
        hosem = [dsem("ho%d" % i) for i in range(4)]
        csem = dsem("const")
        xsem = [dsem("x%d" % i) for i in range(2)]
        tsem = [dsem("t%d" % i) for i in range(2)]
        ropesem = dsem("rope")
        st = {}

        def epoch(tag):
            for e in (PE, ACT, DVE):
                e.new_epoch(sem("%s_%s" % (e.name, tag)))

        epoch("pre")

        def barrier():
            evs = [PE.last, ACT.last, DVE.last]
            for e in (PE, ACT, DVE):
                e.wait(evs)

        units_free = [Unit(u) for u in range(NUNIT)]

        def alloc():
            assert units_free, "out of PSUM units"
            return units_free.pop(0)

        def release(un, evs):
            un.free = evs
            units_free.append(un)

        def uap(un, n=TS, M=128, p0=0, c0=0):
            return ps[p0:p0 + M, un.u * 512 + c0:un.u * 512 + c0 + n]

        wstate = {"idx": 0, "free": [None] * NW}

        def wload(pieces):
            i = wstate["idx"] % NW
            wstate["idx"] += 1
            POOL.wait(wstate["free"][i])
            ev = None
            for (d0, nk, n, src) in pieces:
                ev = POOL.dma(lambda e, o=wbuf[:, i, 0:nk, d0:d0 + n], s=src: e.dma_start(out=o, in_=s), wsem[i])
            return i, ev

        def project(wsrc, nkc, chunks, rhs_fn, T):
            uns = [alloc() for _ in chunks]
            nslab = (nkc + 3) // 4
            done = [None] * len(chunks)
            for s in range(nslab):
                kc0 = 4 * s
                nk = min(4, nkc - kc0)
                i, ev = wload(wsrc(kc0, nk))
                PE.wait(ev)
                last_ev = None
                for kl in range(nk):
                    kc = kc0 + kl
                    for ci, (c0, M) in enumerate(chunks):
                        if kc == 0:
                            PE.wait(uns[ci].free)
                        fin = (kc == nkc - 1)
                        lastmm = (kl == nk - 1) and (ci == len(chunks) - 1)
                        e_ = PE.op(lambda e, o=uap(uns[ci], T, M), l=wbuf[:, i, kl, c0:c0 + M], r=rhs_fn(kc),
                                   s0=(kc == 0), s1=fin: e.matmul(o, lhsT=l, rhs=r, start=s0, stop=s1),
                                   sig=(fin or lastmm))
                        if fin:
                            done[ci] = e_
                        if lastmm:
                            last_ev = e_
                wstate["free"][i] = last_ev
            return list(zip(uns, done))

        def WS(wv_l, kc0, nk, c0, n):
            if _DBG.get("tinyw"):
                return wv_l[:, 0:nk, 0:n]
            return wv_l[:, kc0:kc0 + nk, c0:c0 + n]

        def wsimple(wv_l, c0, n):
            return lambda kc0, nk: [(0, nk, n, WS(wv_l, kc0, nk, c0, n))]

        C4 = [(j * 128, 128) for j in range(4)]

        cev = []
        for (t, d) in ((normg, normg_d), (qlg, qlg_d), (kvg, kvg_d), (qhg, qhg_d), (khg, khg_d), (psc, psc_d),
                       (ident, ident_d)):
            cev.append(SP.dma(lambda e, o=t[:], s=d: e.dma_start(out=o, in_=s), csem))
        cev.append(SP.dma(lambda e: e.dma_start(out=invc[:].rearrange("p a b c -> p (a b c)"), in_=invc_d), csem))
        c0_ev = [DVE.op(lambda e: e.memset(ones[:], 1.0)), DVE.op(lambda e: e.memset(epsb[:], EPS)), DVE.op(lambda e: e.memset(epsk[:], 192.0 * EPS)),
                 DVE.op(lambda e: e.memset(cache[:], 0.0)), DVE.op(lambda e: e.memset(utail[:], 0.0)),
                 DVE.op(lambda e: e.memset(R[:], 0.0)), DVE.op(lambda e: e.memset(winv[:, 0:1], 0.5)), DVE.op(lambda e: e.memset(winv[:, 1:2], 0.25)),
                 DVE.op(lambda e: e.memset(winv[:, 2:3], 0.125)), DVE.op(lambda e: e.memset(winv[:, 3:4], 0.0625)), DVE.op(lambda e: e.memset(S[:], 0.0)), DVE.op(lambda e: e.memset(rk[:], 1.0))]
        for en in (PE, ACT, DVE):
            en.wait(cev[-1])
            en.wait(c0_ev)
        SP.wait(c0_ev)
        POOL.wait(c0_ev)
        ACT.op(lambda e: e.activation(out=epsb[:], in_=epsb[:], func=AF.Copy))
        PE.op(lambda e: e.matmul(ps[:, 0:128], lhsT=ones[:, :], rhs=ones[:, :], start=True, stop=True))

        xstage = Rf[:, 0:2 * 2048].rearrange("p (a f) -> p a f", a=2)
        tstage = Rf[:, 4096:4096 + 2 * 512].rearrange("p (a j t) -> p a j t", a=2, j=4)
        st["x_free"] = [None, None]
        st["t_free"] = [None, None]
        st["xi"] = 0
        st["ti"] = 0
        h_written = []

        def prepass_block(src_rows, ntok, tile, tcol0):
            for half in range(2):
                xi = st["xi"] % 2
                st["xi"] += 1
                lev = SP.dma(lambda e, o=xstage[0:ntok, xi, :], s=src_rows[:, half * 2048:(half + 1) * 2048]: e.dma_start(out=o, in_=s),
                             xsem[xi], deps=st["x_free"][xi])
                lastT = None
                for g4 in range(4):
                    un = alloc()
                    PE.wait(un.free)
                    PE.wait(lev)
                    for j in range(4):
                        f0 = (g4 * 4 + j) * 128
                        lastT = PE.op(lambda e, o=ps[:, un.u * 512 + j * 128:un.u * 512 + j * 128 + ntok],
                                      i_=xstage[0:ntok, xi, f0:f0 + 128], idn=ident[0:ntok, 0:ntok]:
                                      e.transpose(out=o, in_=i_, identity=idn), sig=(j == 3))
                    ti = st["ti"] % 2
                    st["ti"] += 1
                    cev_ = DVE.op(lambda e, o=tstage[:, ti, :, 0:ntok],
                                  i_=ps[:, un.u * 512:(un.u + 1) * 512].rearrange("p (j t) -> p j t", t=128)[:, :, 0:ntok]:
                                  e.tensor_copy(out=o, in_=i_), deps=[lastT, st["t_free"][ti]])
                    release(un, [cev_])
                    kc0 = half * 16 + g4 * 4
                    sev = SP.dma(lambda e, o=hT[0][tile, :, kc0:kc0 + 4, tcol0:tcol0 + ntok], i_=tstage[:, ti, :, 0:ntok]:
                                 e.dma_start(out=o, in_=i_), tsem[ti], deps=cev_)
                    st["t_free"][ti] = sev
                    h_written.append(sev)
                st["x_free"][xi] = lastT

        prepass_block(meta_in[:, :], NM, 0, 0)
        for g in range(ntiles):
            for tb in range(4):
                prepass_block(xin[g * TS + tb * 128:g * TS + (tb + 1) * 128, :], 128, g + 1, tb * 128)
        st["hT_ready"] = list(h_written[-2:])
        SP.wait(st["hT_ready"])
        hring = {"i": 0, "free": [None] * 4}

        def tile(l, ti, src, dst):
            T = NM if ti == 0 else TS
            g = ti - 1
            col0 = 0 if ti == 0 else NM + TS * g
            nkeys = col0 + T
            nkt = 1 if ti == 0 else 1 + 4 * (g + 1)
            wi = w_in_v[l]

            def rhs_hn(kc):
                return hn[:, kc, 0:T]

            barrier()
            rope_ev = SP.dma(lambda e: e.dma_start(out=ropet[:, :, 0:T], in_=rope_d[:, :, col0:col0 + T]), ropesem,
                             deps=[PE.last, ACT.last, DVE.last])

            un = alloc()
            PE.wait(un.free)
            mev = None
            for kc in range(KC):
                hi = hring["i"] % 4
                hring["i"] += 1
                lev = SP.dma(lambda e, o=hst[:, hi, 0:T], s=src[ti, :, kc, 0:T]: e.dma_start(out=o, in_=s), hsem[hi],
                             deps=[hring["free"][hi]] + st["hT_ready"] + [PE.last, ACT.last, DVE.last])
                sq_i = kc % 2
                sev = ACT.op(lambda e, o=sqs[:, sq_i, 0:T], i_=hst[:, hi, 0:T]: e.activation(out=o, in_=i_, func=AF.Square),
                             deps=[lev, st.get("sq_free%d" % sq_i)])
                hring["free"][hi] = sev
                mev = PE.op(lambda e, o=uap(un, T), r=sqs[:, sq_i, 0:T], s0=(kc == 0), s1=(kc == KC - 1):
                            e.matmul(o, lhsT=ones[:, :], rhs=r, start=s0, stop=s1), deps=sev)
                st["sq_free%d" % sq_i] = mev
            a_ = ACT.op(lambda e: e.activation(out=rbc[:, 0:T], in_=uap(un, T), func=AF.Sqrt, bias=epsb[:, 0:1], scale=1.0 / D), deps=mev)
            rev = DVE.op(lambda e: e.reciprocal(out=rbc[:, 0:T], in_=rbc[:, 0:T]), deps=a_)
            release(un, [a_])
            for kc in range(KC):
                hi = hring["i"] % 4
                hring["i"] += 1
                lev = SP.dma(lambda e, o=hst[:, hi, 0:T], s=src[ti, :, kc, 0:T]: e.dma_start(out=o, in_=s), hsem[hi],
                             deps=hring["free"][hi])
                dev = DVE.op(lambda e, o=hn[:, kc, 0:T], i_=hst[:, hi, 0:T], g_=normg[:, l * 32 + kc:l * 32 + kc + 1]:
                             e.scalar_tensor_tensor(out=o, in0=i_, scalar=g_, in1=rbc[:, 0:T], op0=ALU.mult, op1=ALU.mult),
                             deps=[lev, rev])
                hring["free"][hi] = dev
            barrier()

            if _DBG.get('stop') == 'A':
                return
            def norm_group(colbase, nch, gains, gcol0, dst_fn, ndim):
                ssq = alloc()
                PE.wait(ssq.free)
                raw_evs = []
                mev_ = None
                idx = 0
                for g0 in range(0, nch, 4):
                    res = project(wsimple(wi, colbase + g0 * 128, 512), KC, C4, rhs_hn, T)
                    for j, (u_, dn) in enumerate(res):
                        c = g0 + j
                        sq_i = idx % 2
                        idx += 1
                        a1 = ACT.op(lambda e, o=rawst[:, c, 0:T], i_=uap(u_, T): e.activation(out=o, in_=i_, func=AF.Copy), deps=dn)
                        a2 = ACT.op(lambda e, o=sqs[:, sq_i, 0:T], i_=uap(u_, T): e.activation(out=o, in_=i_, func=AF.Square),
                                    deps=[dn, st.get("sq_free%d" % sq_i)])
                        release(u_, [a2])
                        raw_evs.append(a1)
                        mev_ = PE.op(lambda e, o=uap(ssq, T), r=sqs[:, sq_i, 0:T], s0=(c == 0), s1=(c == nch - 1):
                                     e.matmul(o, lhsT=ones[:, :], rhs=r, start=s0, stop=s1), deps=a2)
                        st["sq_free%d" % sq_i] = mev_
                a_ = ACT.op(lambda e: e.activation(out=rbc[:, 0:T], in_=uap(ssq, T), func=AF.Sqrt, bias=epsb[:, 0:1], scale=1.0 / ndim), deps=[mev_, DVE.last])
                r_ev = DVE.op(lambda e: e.reciprocal(out=rbc[:, 0:T], in_=rbc[:, 0:T]), deps=a_)
                release(ssq, [a_])
                for c in range(nch):
                    DVE.op(lambda e, o=dst_fn(c), i_=rawst[:, c, 0:T], g_=gains[:, gcol0 + c:gcol0 + c + 1]:
                           e.scalar_tensor_tensor(out=o, in0=i_, scalar=g_, in1=rbc[:, 0:T], op0=ALU.mult, op1=ALU.mult),
                           deps=[raw_evs[c], r_ev])
                barrier()

            norm_group(OFF["cq"], 8, qlg, l * 8, lambda c: cq[:, c, 0:T], 1024)
            norm_group(OFF["ckv"], 4, kvg, l * 4, lambda c: cache[:, c, col0:col0 + T], 512)

            def wsrc_kr(kc0, nk):
                b = OFF["kr"]
                return [(0, nk, 64, WS(wi, kc0, nk, b, 64)), (64, nk, 64, WS(wi, kc0, nk, b, 64)),
                        (128, nk, 32, WS(wi, kc0, nk, b + 32, 32)), (160, nk, 32, WS(wi, kc0, nk, b, 32))]
            (ux, dx), (uy, dy) = project(wsrc_kr, KC, [(0, 128), (128, 64)], rhs_hn, T)
            t64 = sB[0:64, 0:2, :]
            a1 = DVE.op(lambda e: e.scalar_tensor_tensor(out=t64[:, 0, 0:T], in0=uap(ux, T, 64), scalar=khg[0:64, l * 4 + 1:l * 4 + 2],
                                                         in1=ropet[:, 0, 0:T], op0=ALU.mult, op1=ALU.mult), deps=[dx, rope_ev])
            a2 = DVE.op(lambda e: e.scalar_tensor_tensor(out=t64[:, 1, 0:T], in0=uap(uy, T, 64), scalar=khg[0:64, l * 4 + 2:l * 4 + 3],
                                                         in1=ropet[:, 1, 0:T], op0=ALU.mult, op1=ALU.mult), deps=[dy, rope_ev])
            a3 = DVE.op(lambda e: e.tensor_tensor(out=cache[0:64, 4, col0:col0 + T], in0=t64[:, 0, 0:T], in1=t64[:, 1, 0:T], op=ALU.add),
                        deps=[a1, a2])
            a4 = ACT.op(lambda e: e.activation(out=cache[64:128, 4, col0:col0 + T], in_=uap(ux, T, 64, 64), func=AF.Square), deps=dx)
            release(ux, [a1, a4])
            release(uy, [a2])
            barrier()

            if _DBG.get('stop') == 'B':
                return
            for gp in range(4):
                w_ = WINS[gp]
                c0 = gp * 4
                res = project(wsimple(wi, OFF["pool"] + gp * 512, 512), KC, C4, rhs_hn, T)
                cevs = []
                for j, (u_, dn) in enumerate(res):
                    a_ = ACT.op(lambda e, o=ue[:, 0, j, NM:NM + T], i_=uap(u_, T): e.activation(out=o, in_=i_, func=AF.Copy), deps=dn)
                    b_ = DVE.op(lambda e, o=uo[:, j, 0:T], i_=ue[:, 0, j, NM:NM + T]: e.tensor_copy(out=o, in_=i_), deps=a_)
                    release(u_, [a_])
                    cevs += [a_, b_]
                if _DBG.get('stop') == 'D1':
                    barrier()
                    return
                if ti == 0:
                    hz = DVE.op(lambda e: e.memset(ue[:, 0, :, 0:NM], 0.0))
                else:
                    hz = DVE.op(lambda e, i_=utail[:, c0:c0 + 4, :]: e.tensor_copy(out=ue[:, 0, :, 0:NM], in_=i_))
                cevs.append(hz)
                src_i = 0
                sh = 1
                last = cevs
                for stp in range(gp + 1):
                    nl_ = []
                    for j in range(4):
                        d_ = DVE.op(lambda e, o=ue[:, 1 - src_i, j, sh:NM + T], a=ue[:, src_i, j, sh:NM + T], b__=ue[:, src_i, j, 0:NM + T - sh]:
                                    e.tensor_tensor(out=o, in0=a, in1=b__, op=ALU.add), deps=last)
                        nl_.append(d_)
                    last = nl_
                    src_i = 1 - src_i
                    sh *= 2
                tl = DVE.op(lambda e, o=utail[:, c0:c0 + 4, :]: e.tensor_copy(out=o, in_=uo[:, :, T - NM:T]), deps=cevs)
                if ti == 0:
                    m_ = DVE.op(lambda e, o=ue[:, src_i, :, NM:NM + T], ic=invc[:, gp, :, :]: e.tensor_tensor(out=o, in0=o, in1=ic, op=ALU.mult),
                                deps=last)
                    f_ = DVE.op(lambda e, a=ue[:, src_i, :, NM:NM + T]: e.tensor_tensor(out=diffT[:, :, 0:T], in0=a, in1=uo[:, :, 0:T], op=ALU.subtract),
                                deps=[m_])
                else:
                    for j in range(4):
                        f_ = DVE.op(lambda e, o=diffT[:, j, 0:T], a=ue[:, src_i, j, NM:NM + T], u0=uo[:, j, 0:T], sc=winv[:, gp:gp + 1]:
                                    e.scalar_tensor_tensor(out=o, in0=a, scalar=sc, in1=u0, op0=ALU.mult, op1=ALU.subtract), deps=last)
                if _DBG.get('stop') == 'D2':
                    barrier()
                    return
                res = project(wsimple(wi, OFF["gb"] + gp * 512, 512), KC, C4, rhs_hn, T)
                sg_evs = []
                for j, (u_, dn) in enumerate(res):
                    a_ = ACT.op(lambda e, o=hst[:, j, 0:T], i_=uap(u_, T): e.activation(out=o, in_=i_, func=AF.Silu), deps=dn)
                    release(u_, [a_])
                    sg_evs.append(a_)
                if _DBG.get('stop') == 'D3':
                    barrier()
                    return
                PE.wait(f_)
                res = project(wsimple(w_pool_v[l, gp], 0, 512), 4, C4, lambda kc: diffT[:, kc, 0:T], T)
                for j, (u_, dn) in enumerate(res):
                    d_ = DVE.op(lambda e, o=Gb[:, c0 + j, 0:T], i_=uap(u_, T), s_=psc[:, l * 16 + c0 + j:l * 16 + c0 + j + 1], g_=hst[:, j, 0:T]:
                                e.scalar_tensor_tensor(out=o, in0=i_, scalar=s_, in1=g_, op0=ALU.mult, op1=ALU.mult), deps=[dn, sg_evs[j]])
                    release(u_, [d_])
                barrier()

            if _DBG.get('stop') == 'D':
                return
            qraw = sB[:, 0:3, :]
            rden = sB[:, 3, :]
            kt_cols = [(0, NM)] + [(NM + 128 * j, 128) for j in range(nkt - 1)]

            def prep(h):
                bi = h % 2
                def wsrc_q(kc0, nk):
                    b = h * 192
                    wq = w_uq_v[l]
                    return [(0, nk, 192, WS(wq, kc0, nk, b, 192)), (192, nk, 32, WS(wq, kc0, nk, b + 160, 32)),
                            (224, nk, 32, WS(wq, kc0, nk, b + 128, 32))]
                (un_, dn_), (ur_, dr_), (us_, ds_) = project(wsrc_q, 8, [(0, 128), (128, 64), (192, 64)], lambda kc: cq[:, kc, 0:T], T)
                ssq = alloc()
                PE.wait(ssq.free)
                c1 = ACT.op(lambda e: e.activation(out=qraw[:, 0, 0:T], in_=uap(un_, T), func=AF.Copy), deps=[dn_, st.get("qraw_free")])
                s1 = ACT.op(lambda e: e.activation(out=sqs[:, 0, 0:T], in_=uap(un_, T), func=AF.Square), deps=[dn_, st.get("sq_free0")])
                c2 = ACT.op(lambda e: e.activation(out=qraw[0:64, 1, 0:T], in_=uap(ur_, T, 64), func=AF.Copy), deps=dr_)
                s2 = ACT.op(lambda e: e.activation(out=sqs[0:64, 1, 0:T], in_=uap(ur_, T, 64), func=AF.Square), deps=[dr_, st.get("sq_free1")])
                c3 = ACT.op(lambda e: e.activation(out=qraw[0:64, 2, 0:T], in_=uap(us_, T, 64), func=AF.Copy), deps=ds_)
                release(un_, [s1])
                release(ur_, [s2])
                release(us_, [c3])
                PE.op(lambda e: e.matmul(uap(ssq, T), lhsT=ones[:, :], rhs=sqs[:, 0, 0:T], start=True, stop=False), sig=False, deps=s1)
                m_ = PE.op(lambda e: e.matmul(uap(ssq, T), lhsT=ones[0:64, :], rhs=sqs[0:64, 1, 0:T], start=False, stop=True), deps=s2)
                st["sq_free0"] = m_
                st["sq_free1"] = m_
                a_ = ACT.op(lambda e: e.activation(out=rbc[:, 0:T], in_=uap(ssq, T), func=AF.Sqrt, bias=epsb[:, 0:1], scale=1.0 / 192), deps=[m_, DVE.last])
                r_ = DVE.op(lambda e: e.reciprocal(out=rbc[:, 0:T], in_=rbc[:, 0:T]), deps=a_)
                release(ssq, [a_])
                qf = st.get("Q_free%d" % bi)
                q1 = DVE.op(lambda e: e.scalar_tensor_tensor(out=Qn[:, bi, 0:T], in0=qraw[:, 0, 0:T], scalar=qhg[:, l * 4:l * 4 + 1], in1=rbc[:, 0:T],
                                                             op0=ALU.mult, op1=ALU.mult), deps=[c1, r_, qf])
                t1 = DVE.op(lambda e: e.scalar_tensor_tensor(out=qraw[0:64, 1, 0:T], in0=qraw[0:64, 1, 0:T], scalar=qhg[0:64, l * 4 + 1:l * 4 + 2],
                                                             in1=ropet[:, 0, 0:T], op0=ALU.mult, op1=ALU.mult), deps=[c2, rope_ev])
                t2 = DVE.op(lambda e: e.scalar_tensor_tensor(out=qraw[0:64, 2, 0:T], in0=qraw[0:64, 2, 0:T], scalar=qhg[0:64, l * 4 + 2:l * 4 + 3],
                                                             in1=ropet[:, 1, 0:T], op0=ALU.mult, op1=ALU.mult), deps=[c3, t1])
                t3 = DVE.op(lambda e: e.tensor_tensor(out=qraw[0:64, 1, 0:T], in0=qraw[0:64, 1, 0:T], in1=qraw[0:64, 2, 0:T], op=ALU.add), deps=t2)
                q2 = DVE.op(lambda e: e.tensor_tensor(out=Qr[:, bi, 0:T], in0=qraw[0:64, 1, 0:T], in1=rbc[0:64, 0:T], op=ALU.mult), deps=t3)
                st["qraw_free"] = q2
                i, wev = wload([(0, 4, 256, WS(w_ukv_v[l], 0, 4, h * 256, 256))])
                PE.wait(wev)
                kvf = st.get("KV_free%d" % bi)
                PE.wait(kvf)
                rku = alloc()
                PE.wait(rku.free)
                k_evs = []
                lastmm = None
                seg_starts = [0] + list(range(400, nkeys, 512))
                for sidx, s0 in enumerate(seg_starts):
                    n = min(400 if s0 == 0 else 512, nkeys - s0)
                    un2 = alloc()
                    PE.wait(un2.free)
                    for kc in range(4):
                        lastmm = PE.op(lambda e, o=uap(un2, n), l_=wbuf[:, i, kc, 0:128], r=cache[:, kc, s0:s0 + n], a=(kc == 0), b=(kc == 3):
                                       e.matmul(o, lhsT=l_, rhs=r, start=a, stop=b), sig=(kc == 3))
                    d_ = DVE.op(lambda e, o=Kh[bi][:, s0:s0 + n], i_=uap(un2, n): e.tensor_scalar(out=o, in0=i_, scalar1=khg[:, l * 4:l * 4 + 1], scalar2=0.0, op0=ALU.mult, op1=ALU.add),
                                deps=[lastmm, kvf])
                    sqi = sidx % 2
                    a_ = ACT.op(lambda e, o=sqs[:, sqi, 0:n], i_=uap(un2, n): e.activation(out=o, in_=i_, func=AF.Square),
                                deps=[lastmm, d_, st.get("sq_free%d" % sqi)])
                    release(un2, [d_, a_])
                    k_evs.append(d_)
                    mm_ = None
                    for kt, (kc0_, nk_) in enumerate(kt_cols):
                        if kc0_ < s0 or kc0_ >= s0 + n:
                            continue
                        PE.op(lambda e, o=ps[0:nk_, rku.u * 512 + kt:rku.u * 512 + kt + 1], l_=sqs[:, sqi, kc0_ - s0:kc0_ - s0 + nk_]:
                              e.matmul(o, lhsT=l_, rhs=ones[:, 0:1], start=True, stop=False), sig=False, deps=a_)
                        mm_ = PE.op(lambda e, o=ps[0:nk_, rku.u * 512 + kt:rku.u * 512 + kt + 1], l_=cache[64:128, 4, kc0_:kc0_ + nk_]:
                                    e.matmul(o, lhsT=l_, rhs=ones[64:128, 0:1], start=False, stop=True), sig=False)
                    lastmm = PE.op(lambda e: e.matmul(ps[0:1, rku.u * 512 + 64:rku.u * 512 + 65], lhsT=ones[0:1, 0:1], rhs=ones[0:1, 0:1], start=True, stop=True))
                    st["sq_free%d" % sqi] = lastmm
                a_ = ACT.op(lambda e: e.activation(out=rk[0:NM, bi, 0:1], in_=ps[0:NM, rku.u * 512:rku.u * 512 + 1], func=AF.Sqrt, bias=epsk[0:NM, 0:1], scale=1.0),
                            deps=[lastmm, kvf])
                if nkt > 1:
                    a_ = ACT.op(lambda e: e.activation(out=rk[:, bi, 1:nkt], in_=ps[:, rku.u * 512 + 1:rku.u * 512 + nkt], func=AF.Sqrt, bias=epsk[:, 0:1], scale=1.0),
                                deps=[lastmm, kvf])
                r2 = DVE.op(lambda e: e.reciprocal(out=rk[:, bi, 0:nkt], in_=rk[:, bi, 0:nkt]), deps=a_)
                release(rku, [a_])
                v_evs = []
                for k0 in range(0, nkt, 4):
                    nn = min(4, nkt - k0)
                    un2 = alloc()
                    PE.wait(un2.free)
                    for jj in range(nn):
                        kc0_, nk_ = kt_cols[k0 + jj]
                        for kc in range(4):
                            lastmm = PE.op(lambda e, o=ps[0:nk_, un2.u * 512 + jj * 128:un2.u * 512 + jj * 128 + 128], l_=cache[:, kc, kc0_:kc0_ + nk_],
                                           r=wbuf[:, i, kc, 128:256], a=(kc == 0), b=(kc == 3): e.matmul(o, lhsT=l_, rhs=r, start=a, stop=b),
                                           sig=(kc == 3 and jj == nn - 1))
                    if k0 == 0:
                        pass
                    if k0 == 0:
                        c0_ = ACT.op(lambda e, o=Vh[bi][0:NM, 0, :], i_=ps[0:NM, un2.u * 512:un2.u * 512 + 128]:
                                     e.activation(out=o, in_=i_, func=AF.Copy), deps=[lastmm, kvf])
                        c_ = c0_
                        if nn > 1:
                            c_ = ACT.op(lambda e, o=Vh[bi][:, 1:nn, :], i_=ps[:, un2.u * 512 + 128:un2.u * 512 + nn * 128].rearrange("p (k d) -> p k d", d=128):
                                        e.activation(out=o, in_=i_, func=AF.Copy), deps=[lastmm, kvf])
                    else:
                        c_ = ACT.op(lambda e, o=Vh[bi][:, k0:k0 + nn, :], i_=ps[:, un2.u * 512:un2.u * 512 + nn * 128].rearrange("p (k d) -> p k d", d=128):
                                    e.activation(out=o, in_=i_, func=AF.Copy), deps=[lastmm, kvf])
                    release(un2, [c_])
                    v_evs.append(c_)
                wstate["free"][i] = lastmm
                return dict(q=[q1, q2], k=k_evs, v=v_evs, rk=r2)

            def attend(h, pe_):
                bi = h % 2
                PE.wait(pe_["q"])
                PE.wait(pe_["k"])
                PE.wait(pe_["v"])
                ou = alloc()
                du = alloc()
                PE.wait(ou.free)
                PE.wait(du.free)
                last_pv = None
                last_exp = None
                for kt, (kc0_, nk_) in enumerate(kt_cols):
                    own = (ti > 0) and (kt >= 1 + 4 * g)
                    jj = (kt - 1 - 4 * g) if own else 0
                    qc0 = 128 * jj if own else 0
                    su = alloc()
                    PE.wait(su.free)
                    PE.op(lambda e, o=ps[0:nk_, su.u * 512 + qc0:su.u * 512 + T], l_=Kh[bi][:, kc0_:kc0_ + nk_], r=Qn[:, bi, qc0:T]:
                          e.matmul(o, lhsT=l_, rhs=r, start=True, stop=False), sig=False)
                    sm = PE.op(lambda e, o=ps[0:nk_, su.u * 512 + qc0:su.u * 512 + T], l_=cache[0:64, 4, kc0_:kc0_ + nk_], r=Qr[:, bi, qc0:T]:
                               e.matmul(o, lhsT=l_, rhs=r, start=False, stop=True))
                    pi = kt % 3
                    ex = ACT.op(lambda e, o=pbuf[0:nk_, pi, qc0:T], i_=ps[0:nk_, su.u * 512 + qc0:su.u * 512 + T], sc=rk[0:nk_, bi, kt:kt + 1]:
                                e.activation(out=o, in_=i_, func=AF.Exp, scale=sc), deps=[sm, pe_["rk"], st.get("p_free%d" % pi)])
                    release(su, [ex])
                    last_exp = ex
                    PE.wait(ex)
                    ranges = [(qc0, T, nk_)]
                    if own:
                        ranges = [(qc0 + 64, T, nk_), (qc0, qc0 + 64, 64)]
                    ranges = [rg for rg in ranges if rg[1] > rg[0]]
                    for ri, (a, b, kk) in enumerate(ranges):
                        fin_ = (kt == nkt - 1) and (ri == len(ranges) - 1)
                        PE.op(lambda e, o=ps[:, ou.u * 512 + a:ou.u * 512 + b], l_=Vh[bi][0:kk, kt, :], r=pbuf[0:kk, pi, a:b], s0=(kt == 0), s1=fin_:
                              e.matmul(o, lhsT=l_, rhs=r, start=s0, stop=s1), sig=False)
                        last_pv = PE.op(lambda e, o=ps[:, du.u * 512 + a:du.u * 512 + b], l_=ones[0:kk, :], r=pbuf[0:kk, pi, a:b], s0=(kt == 0), s1=fin_:
                                        e.matmul(o, lhsT=l_, rhs=r, start=s0, stop=s1))
                    st["p_free%d" % pi] = last_pv
                st["KV_free%d" % bi] = last_pv
                st["Q_free%d" % bi] = last_pv
                r_ = DVE.op(lambda e: e.reciprocal(out=rden[:, 0:T], in_=uap(du, T)), deps=last_pv)
                o1 = DVE.op(lambda e: e.tensor_tensor(out=rden[:, 0:T], in0=uap(ou, T), in1=rden[:, 0:T], op=ALU.mult), deps=r_)
                o2 = DVE.op(lambda e: e.tensor_tensor(out=Ga[:, h, 0:T], in0=rden[:, 0:T], in1=hst[:, h % 4, 0:T], op=ALU.mult),
                            deps=[o1, st["sg_ev"][h % 4]])
                release(ou, [o1])
                release(du, [r_])
                return o2

            for h4 in range(4):
                res = project(wsimple(wi, OFF["ga"] + h4 * 512, 512), KC, C4, rhs_hn, T)
                st["sg_ev"] = []
                for j, (u_, dn) in enumerate(res):
                    a_ = ACT.op(lambda e, o=hst[:, j, 0:T], i_=uap(u_, T): e.activation(out=o, in_=i_, func=AF.Silu), deps=[dn, DVE.last])
                    release(u_, [a_])
                    st["sg_ev"].append(a_)
                for hh in range(4):
                    h = h4 * 4 + hh
                    pe_ = prep(h)
                    attend(h, pe_)
                barrier()

            if _DBG.get('stop') == 'C':
                return
            for j4 in range(8):
                res = project(wsimple(wi, OFF["ma"] + j4 * 512, 512), KC, C4, rhs_hn, T)
                sa_ev = []
                for j, (u_, dn) in enumerate(res):
                    a_ = ACT.op(lambda e, o=hst[:, j, 0:T], i_=uap(u_, T): e.activation(out=o, in_=i_, func=AF.Sigmoid), deps=[dn, DVE.last])
                    release(u_, [a_])
                    sa_ev.append(a_)
                res = project(wsimple(wi, OFF["mb"] + j4 * 512, 512), KC, C4, rhs_hn, T)
                sb_ev = []
                for j, (u_, dn) in enumerate(res):
                    a_ = ACT.op(lambda e, o=sB[:, j, 0:T], i_=uap(u_, T): e.activation(out=o, in_=i_, func=AF.Sigmoid), deps=[dn, DVE.last])
                    release(u_, [a_])
                    sb_ev.append(a_)
                res = project(wsimple(w_ba_v[l], j4 * 512, 512), 16, C4, lambda kc: Ga[:, kc, 0:T], T)
                ta_ev = []
                for j, (u_, dn) in enumerate(res):
                    d_ = DVE.op(lambda e, o=hst[:, j, 0:T], i_=uap(u_, T): e.tensor_tensor(out=o, in0=i_, in1=o, op=ALU.mult), deps=[dn, sa_ev[j]])
                    release(u_, [d_])
                    ta_ev.append(d_)
                res = project(wsimple(w_bb_v[l], j4 * 512, 512), 16, C4, lambda kc: Gb[:, kc, 0:T], T)
                for j, (u_, dn) in enumerate(res):
                    d_ = DVE.op(lambda e, o=sB[:, j, 0:T], i_=uap(u_, T): e.tensor_tensor(out=o, in0=i_, in1=o, op=ALU.mult), deps=[dn, sb_ev[j]])
                    release(u_, [d_])
                    DVE.op(lambda e, o=merged[:, j4 * 4 + j, 0:T], a=hst[:, j, 0:T], b=sB[:, j, 0:T]: e.tensor_tensor(out=o, in0=a, in1=b, op=ALU.add),
                           deps=[d_, ta_ev[j]])
                barrier()

            if _DBG.get('stop') == 'F':
                return
            for o4 in range(8):
                hl = []
                for j in range(4):
                    oc = o4 * 4 + j
                    lev = SP.dma(lambda e, o=hst[:, j, 0:T], s=src[ti, :, oc, 0:T]: e.dma_start(out=o, in_=s), hsem[j],
                                 deps=[st.get("ho_ev%d" % j), DVE.last, ACT.last])
                    hl.append(lev)
                res = project(wsimple(w_out_v[l], o4 * 512, 512), KC, C4, lambda kc: merged[:, kc, 0:T], T)
                for j, (u_, dn) in enumerate(res):
                    oc = o4 * 4 + j
                    d_ = DVE.op(lambda e, o=hst[:, j, 0:T], i_=uap(u_, T): e.tensor_tensor(out=o, in0=i_, in1=o, op=ALU.add), deps=[dn, hl[j]])
                    release(u_, [d_])
                    sev = SP.dma(lambda e, o=dst[ti, :, oc, 0:T], i_=hst[:, j, 0:T]: e.dma_start(out=o, in_=i_), hosem[j], deps=d_)
                    st["ho_ev%d" % j] = sev
                    st["last_store"] = sev
                barrier()
            for j in range(4):
                hring["free"][j] = st.get("ho_ev%d" % j)

        cur = 0
        for li, l in enumerate(layers):
            epoch("L%d" % li)
            if li > 0:
                for i_ in range(NW):
                    wsem[i_] = dsem("w%d_%d" % (i_, li))
                for i_ in range(4):
                    hsem[i_] = dsem("h%d_%d" % (i_, li))
                    hosem[i_] = dsem("ho%d_%d" % (i_, li))
            for ti in range(ntiles + 1):
                tile(l, ti, hT[cur], hT[1 - cur])
            st["hT_ready"] = [st.get("ho_ev%d" % j) for j in range(4)]
            cur = 1 - cur

        barrier()
        SP.wait(st["hT_ready"])
        ostage = Rf[:, 0:2 * 4096].rearrange("p (a f) -> p a f", a=2)
        fst = S[:, 0:2 * 4 * 128].rearrange("p (a j t) -> p a j t", a=2, j=4)
        osem = [dsem("os%d" % i) for i in range(2)]
        fsem = [dsem("fs%d" % i) for i in range(2)]
        st["o_free"] = [None, None]
        st["f_free"] = [None, None]
        fin_evs = []
        blk = 0
        fi = 0
        for ti in range(ntiles + 1):
            ntok_all = NM if ti == 0 else TS
            for tb in range(0, ntok_all, 128):
                ntok = min(128, ntok_all - tb)
                oi = blk % 2
                blk += 1
                cp = None
                for g4 in range(8):
                    fb = fi % 2
                    fi += 1
                    lev = SP.dma(lambda e, o=fst[:, fb, :, 0:ntok], s=hT[cur][ti, :, g4 * 4:g4 * 4 + 4, tb:tb + ntok]: e.dma_start(out=o, in_=s),
                                 fsem[fb], deps=[st["f_free"][fb], PE.last, DVE.last, ACT.last] if g4 < 2 and blk == 1 else st["f_free"][fb])
                    un = alloc()
                    PE.wait(un.free)
                    PE.wait(lev)
                    lt = None
                    for j in range(4):
                        lt = PE.op(lambda e, o=ps[0:ntok, un.u * 512 + j * 128:un.u * 512 + (j + 1) * 128], i_=fst[:, fb, j, 0:ntok]:
                                   e.transpose(out=o, in_=i_, identity=ident[:, :]), sig=(j == 3))
                    st["f_free"][fb] = lt
                    cp = DVE.op(lambda e, o=ostage[0:ntok, oi, g4 * 512:(g4 + 1) * 512], i_=ps[0:ntok, un.u * 512:(un.u + 1) * 512]:
                                e.tensor_copy(out=o, in_=i_), deps=[lt, st["o_free"][oi]])
                    release(un, [cp])
                row0 = NT * TS if ti == 0 else (ti - 1) * TS + tb
                sev = SP.dma(lambda e, o=out_d[row0:row0 + ntok, :], i_=ostage[0:ntok, oi, :]: e.dma_start(out=o, in_=i_), osem[oi], deps=cp)
                st["o_free"][oi] = sev
                fin_evs.append(sev)
        SP.wait(fin_evs[-2:])

        _DBG['engs'] = (PE, ACT, DVE, POOL, SP)
        if _DBG.get('nosim') is None:
            _deadlock_check((PE, ACT, DVE, POOL, SP))
        with nc.Block() as block:
            @block.tensor
            def _(e):
                PE.replay(e)

            @block.scalar
            def _(e):
                ACT.replay(e)

            @block.vector
            def _(e):
                DVE.replay(e)

            @block.gpsimd
            def _(e):
                POOL.replay(e)

            @block.sync
            def _(e):
                SP.replay(e)
    return nc


def _host_consts():
    pos = np.arange(KEYS, dtype=np.float32)
    inv = (1.0 / (np.float32(10000.0) ** (np.arange(0, 64, 2, dtype=np.float32) / np.float32(64)))).astype(np.float32)
    ang = (pos[:, None] * inv[None, :]).astype(np.float32)
    cos = np.cos(ang).astype(np.float32).T
    sin = np.sin(ang).astype(np.float32).T
    rope = np.zeros((64, 2, KEYS), np.float32)
    rope[0:32, 0] = cos
    rope[32:64, 0] = cos
    rope[0:32, 1] = -sin
    rope[32:64, 1] = sin
    invc = np.zeros((128, 4, 4, NM), np.float32)
    p = np.arange(NM, dtype=np.float32)
    for gi, w in enumerate(WINS):
        invc[:, gi, :, :] = (1.0 / np.minimum(p + 1.0, np.float32(w))).astype(np.float32)[None, None, :]
    return rope, invc.reshape(128, -1), np.eye(128, dtype=np.float32)


def _pp(v, nch):
    return np.ascontiguousarray(v.reshape(DEPTH, nch, 128).transpose(2, 0, 1).reshape(128, DEPTH * nch))


def _head_gain(g):
    o = np.zeros((128, DEPTH, 4), np.float32)
    for l in range(DEPTH):
        o[:, l, 0] = g[l, 0:128]
        o[0:64, l, 1] = g[l, 128:192]
        o[0:32, l, 2] = g[l, 160:192]
        o[32:64, l, 2] = g[l, 128:160]
    return np.ascontiguousarray(o.reshape(128, DEPTH * 4))


_NC_CACHE = {}


def _launch(layers, x_cores, meta_cores, weights, consts):
    key = tuple(layers)
    if key not in _NC_CACHE:
        _NC_CACHE[key] = build_program(list(layers))
    nc = _NC_CACHE[key]
    in_maps = []
    for c in range(8):
        m = dict(weights)
        m.update(consts)
        m["xin"] = x_cores[c % 4]
        m["meta"] = meta_cores[c % 4]
        in_maps.append(m)
    res = run_bass_kernel_spmd(nc, in_maps, core_ids=list(range(8)))
    return [res.results[c]["out"] for c in range(4)]


def kernel(x, meta_tokens, norm_g, w_in, q_lora_g, kv_lora_g, w_uq, w_ukv, q_head_g, k_head_g,
           w_pool, pool_scale, w_branch_a, w_branch_b, w_out):
    f = lambda a: np.ascontiguousarray(np.asarray(a, dtype=np.float32))
    x = f(x)
    rope, invc, ident = _host_consts()
    weights = dict(w_in=f(w_in), w_uq=f(w_uq), w_ukv=f(w_ukv), w_pool=f(w_pool), w_branch_a=f(w_branch_a),
                   w_branch_b=f(w_branch_b), w_out=f(w_out))
    consts = dict(normg=_pp(f(norm_g), 32), qlg=_pp(f(q_lora_g), 8), kvg=_pp(f(kv_lora_g), 4),
                  qhg=_head_gain(f(q_head_g)), khg=_head_gain(f(k_head_g)), pscale=_pp(f(pool_scale), 16),
                  ropet=rope, invc=invc, ident=ident)
    xc = [x[b] for b in range(4)]
    mc = [f(meta_tokens) for _ in range(4)]
    outs = _launch(tuple(range(DEPTH)), xc, mc, weights, consts)
    return np.stack([o[0:NT * TS] for o in outs], axis=0)
```

```python
import contextlib
import numpy as np
import concourse.bass as bass
import concourse.mybir as mybir
from concourse.bass_utils import run_bass_kernel_spmd

F32 = mybir.dt.float32
BF16 = mybir.dt.bfloat16
AF = mybir.ActivationFunctionType
ALU = mybir.AluOpType

D = 4096
KC = 32
DEPTH = 4
NT = 8
TS = 512
NM = 16
IN_W = 15936
OFF = dict(cq=0, ckv=1024, kr=1536, pool=1600, ga=3648, gb=5696, ma=7744, mb=11840)
EPS = 1e-6
NW = 5
NUNIT = 8
KEYS = NM + NT * TS
NKT = 1 + 32
WINS = (2, 4, 8, 16)
RB = 33344
SBY = 18432


class Ev:
    __slots__ = ("sem", "val")

    def __init__(self, sem, val):
        self.sem = sem
        self.val = val


class Eng:
    def __init__(self, name):
        self.name = name
        self.ops = []
        self.sem = None
        self.cnt = 0
        self.waited = {}
        self.last = None

    def new_epoch(self, sem):
        self.sem = sem
        self.cnt = 0

    def wait(self, ev):
        if ev is None:
            return
        if isinstance(ev, (list, tuple)):
            for e in ev:
                self.wait(e)
            return
        k = id(ev.sem)
        if self.waited.get(k, 0) >= ev.val:
            return
        self.waited[k] = ev.val
        self.ops.append(("w", ev.sem, ev.val))

    def op(self, fn, sig=True, deps=None):
        self.wait(deps)
        if sig:
            self.cnt += 1
            self.ops.append(("o", fn, self.sem, 1))
            self.last = Ev(self.sem, self.cnt)
            return self.last
        self.ops.append(("o", fn, None, 0))
        return None

    def dma(self, fn, dsem, deps=None):
        self.wait(deps)
        dsem[1] += 16
        self.ops.append(("o", fn, dsem[0], 16))
        return Ev(dsem[0], dsem[1])

    def replay(self, e):
        for o in self.ops:
            if o[0] == "w":
                e.wait_ge(o[1], o[2])
            else:
                ins = o[1](e)
                if o[2] is not None:
                    ins.then_inc(o[2], o[3])


class Unit:
    __slots__ = ("u", "free")

    def __init__(self, u):
        self.u = u
        self.free = None


_DBG = {}


def _deadlock_check(engs):
    sems = {}
    pcs = [0] * len(engs)
    progress = True
    while progress:
        progress = False
        for i, en in enumerate(engs):
            while pcs[i] < len(en.ops):
                o = en.ops[pcs[i]]
                if o[0] == "w":
                    if sems.get(id(o[1]), 0) >= o[2]:
                        pcs[i] += 1
                        progress = True
                    else:
                        break
                else:
                    if o[2] is not None:
                        sems[id(o[2])] = sems.get(id(o[2]), 0) + o[3]
                    pcs[i] += 1
                    progress = True
    stuck = [(en.name, pcs[i], len(en.ops)) for i, en in enumerate(engs) if pcs[i] < len(en.ops)]
    if stuck:
        msg = []
        for i, en in enumerate(engs):
            if pcs[i] < len(en.ops):
                o = en.ops[pcs[i]]
                msg.append("%s pc=%d/%d waits val=%d have=%d" % (en.name, pcs[i], len(en.ops), o[2], sems.get(id(o[1]), 0)))
        raise RuntimeError("DEADLOCK: " + "; ".join(msg))


def build_program(layers, wdepth=DEPTH, ntiles=NT):
    nc = bass.Bass("TRN2", target_bir_lowering=False)

    def din(name, shape):
        return nc.dram_tensor(name, list(shape), F32, kind="ExternalInput").ap()

    xin = din("xin", [NT * TS, D])
    meta_in = din("meta", [NM, D])
    w_in = din("w_in", [wdepth, D, IN_W] if not _DBG.get("tinyw") else [1, 512, 512])
    w_uq = din("w_uq", [wdepth, 1024, 3072] if not _DBG.get("tinyw") else [1, 512, 512])
    w_ukv = din("w_ukv", [wdepth, 512, 4096] if not _DBG.get("tinyw") else [1, 512, 512])
    w_pool = din("w_pool", [wdepth, 4, 512, 512])
    w_ba = din("w_branch_a", [wdepth, 2048, D] if not _DBG.get("tinyw") else [1, 512, 512])
    w_bb = din("w_branch_b", [wdepth, 2048, D] if not _DBG.get("tinyw") else [1, 512, 512])
    w_out = din("w_out", [wdepth, D, D] if not _DBG.get("tinyw") else [1, 512, 512])
    normg_d = din("normg", [128, DEPTH * 32])
    qlg_d = din("qlg", [128, DEPTH * 8])
    kvg_d = din("kvg", [128, DEPTH * 4])
    qhg_d = din("qhg", [128, DEPTH * 4])
    khg_d = din("khg", [128, DEPTH * 4])
    psc_d = din("pscale", [128, DEPTH * 16])
    rope_d = din("ropet", [64, 2, KEYS])
    invc_d = din("invc", [128, 4 * 4 * NM])
    ident_d = din("ident", [128, 128])
    out_d = nc.dram_tensor("out", [NT * TS + NM, D], F32, kind="ExternalOutput").ap()
    hT = [nc.dram_tensor("hT%d" % i, [NT + 1, 128, KC, TS], F32) for i in range(2)]

    def kview(w):
        return w.rearrange("l (kc p) c -> l p kc c", p=128)

    w_in_v, w_uq_v, w_ukv_v = kview(w_in), kview(w_uq), kview(w_ukv)
    w_ba_v, w_bb_v, w_out_v = kview(w_ba), kview(w_bb), kview(w_out)
    w_pool_v = w_pool.rearrange("l g (kc p) c -> l g p kc c", p=128)

    PE, ACT, DVE, POOL, SP = Eng("pe"), Eng("act"), Eng("dve"), Eng("pool"), Eng("sp")

    with contextlib.ExitStack() as es:
        def sb(name, shape, dt):
            return es.enter_context(nc.sbuf_tensor(name, list(shape), dt))

        def sem(name):
            return es.enter_context(nc.semaphore(name))

        def dsem(name):
            return [sem(name), 0]

        hn = sb("hn", [128, KC, TS], BF16)
        cq = sb("cq", [128, 8, TS], BF16)
        Ga = sb("Ga", [128, 16, TS], BF16)
        Gb = sb("Gb", [128, 16, TS], BF16)
        R = sb("R", [128, RB // 2], BF16)
        S = sb("S", [128, SBY // 4], F32)
        cache = sb("cache", [128, 5, KEYS], BF16)
        wbuf = sb("wbuf", [128, NW, 4, 512], BF16)
        ropet = sb("ropet_s", [64, 2, TS], F32)
        normg = sb("normg_s", [128, DEPTH * 32], F32)
        qlg = sb("qlg_s", [128, DEPTH * 8], F32)
        kvg = sb("kvg_s", [128, DEPTH * 4], F32)
        qhg = sb("qhg_s", [128, DEPTH * 4], F32)
        khg = sb("khg_s", [128, DEPTH * 4], F32)
        psc = sb("psc", [128, DEPTH * 16], F32)
        invc = sb("invc_s", [128, 4, 4, NM], F32)
        ident = sb("ident_s", [128, 128], F32)
        ones = sb("ones", [128, 128], BF16)
        epsb = sb("epsb", [128, 1], F32)
        epsk = sb("epsk", [128, 1], F32)
        winv = sb("winv", [128, 4], F32)
        sqs = sb("sqs", [128, 2, TS], BF16)
        Qn = sb("Qn", [128, 2, TS], BF16)
        Qr = sb("Qr", [64, 2, TS], BF16)
        rk = sb("rk", [128, 2, NKT + 1], F32)
        pbuf = sb("pbuf", [128, 3, TS], BF16)
        utail = sb("utail", [128, 16, NM], F32)
        ps = es.enter_context(nc.psum_tensor("ps", [128, 4096], F32))

        Rf = R[:, :].bitcast(F32)
        merged = R[:, 0:KC * TS].rearrange("p (c t) -> p c t", t=TS)
        KVS = RB // 4
        Kh = [R[:, i * KVS:i * KVS + KEYS] for i in range(2)]
        Vh = [R[:, i * KVS + KEYS:i * KVS + KEYS + NKT * 128].rearrange("p (k d) -> p k d", d=128) for i in range(2)]
        rawst = Rf[:, 0:8 * TS].rearrange("p (c t) -> p c t", t=TS)
        UE = TS + NM
        ue = Rf[:, 0:2 * 4 * UE].rearrange("p (a c t) -> p a c t", a=2, c=4)
        uo = Rf[:, 2 * 4 * UE:2 * 4 * UE + 4 * TS].rearrange("p (c t) -> p c t", t=TS)
        dT0 = (2 * 4 * UE + 4 * TS) * 2
        diffT = R[:, dT0:dT0 + 4 * TS].rearrange("p (c t) -> p c t", t=TS)
        hst = S[:, 0:4 * TS].rearrange("p (c t) -> p c t", t=TS)
        sB = S[:, 4 * TS:8 * TS].rearrange("p (c t) -> p c t", t=TS)
        rbc = S[:, 8 * TS:9 * TS]

        wsem = [dsem("w%d" % i) for i in range(NW)]
        hsem = [dsem("h%d" % i) for i in range(4)]
        hosem = [dsem("ho%d" % i) for i in range(4)]
        csem = dsem("const")
        xsem = [dsem("x%d" % i) for i in range(2)]
        tsem = [dsem("t%d" % i) for i in range(2)]
        ropesem = dsem("rope")
        st = {}

        def epoch(tag):
            for e in (PE, ACT, DVE):
                e.new_epoch(sem("%s_%s" % (e.name, tag)))

        epoch("pre")

        def barrier():
            evs = [PE.last, ACT.last, DVE.last]
            for e in (PE, ACT, DVE):
                e.wait(evs)

        units_free = [Unit(u) for u in range(NUNIT)]

        def alloc():
            assert units_free, "out of PSUM units"
            return units_free.pop(0)

        def release(un, evs):
            un.free = evs
            units_free.append(un)

        def uap(un, n=TS, M=128, p0=0, c0=0):
            return ps[p0:p0 + M, un.u * 512 + c0:un.u * 512 + c0 + n]

        wstate = {"idx": 0, "free": [None] * NW}

        def wload(pieces):
            i = wstate["idx"] % NW
            wstate["idx"] += 1
            POOL.wait(wstate["free"][i])
            ev = None
            for (d0, nk, n, src) in pieces:
                ev = POOL.dma(lambda e, o=wbuf[:, i, 0:nk, d0:d0 + n], s=src: e.dma_start(out=o, in_=s), wsem[i])
            return i, ev

        def project(wsrc, nkc, chunks, rhs_fn, T):
            uns = [alloc() for _ in chunks]
            nslab = (nkc + 3) // 4
            done = [None] * len(chunks)
            for s in range(nslab):
                kc0 = 4 * s
                nk = min(4, nkc - kc0)
                i, ev = wload(wsrc(kc0, nk))
                PE.wait(ev)
                last_ev = None
                for kl in range(nk):
                    kc = kc0 + kl
                    for ci, (c0, M) in enumerate(chunks):
                        if kc == 0:
                            PE.wait(uns[ci].free)
                        fin = (kc == nkc - 1)
                        lastmm = (kl == nk - 1) and (ci == len(chunks) - 1)
                        e_ = PE.op(lambda e, o=uap(uns[ci], T, M), l=wbuf[:, i, kl, c0:c0 + M], r=rhs_fn(kc),
                                   s0=(kc == 0), s1=fin: e.matmul(o, lhsT=l, rhs=r, start=s0, stop=s1),
                                   sig=(fin or lastmm))
                        if fin:
                            done[ci] = e_
                        if lastmm:
                            last_ev = e_
                wstate["free"][i] = last_ev
            return list(zip(uns, done))

        def WS(wv_l, kc0, nk, c0, n):
            if _DBG.get("tinyw"):
                return wv_l[:, 0:nk, 0:n]
            return wv_l[:, kc0:kc0 + nk, c0:c0 + n]

        def wsimple(wv_l, c0, n):
            return lambda kc0, nk: [(0, nk, n, WS(wv_l, kc0, nk, c0, n))]

        C4 = [(j * 128, 128) for j in range(4)]

        cev = []
        for (t, d) in ((normg, normg_d), (qlg, qlg_d), (kvg, kvg_d), (qhg, qhg_d), (khg, khg_d), (psc, psc_d),
                       (ident, ident_d)):
            cev.append(SP.dma(lambda e, o=t[:], s=d: e.dma_start(out=o, in_=s), csem))
        cev.append(SP.dma(lambda e: e.dma_start(out=invc[:].rearrange("p a b c -> p (a b c)"), in_=invc_d), csem))
        c0_ev = [DVE.op(lambda e: e.memset(ones[:], 1.0)), DVE.op(lambda e: e.memset(epsb[:], EPS)), DVE.op(lambda e: e.memset(epsk[:], 192.0 * EPS)),
                 DVE.op(lambda e: e.memset(cache[:], 0.0)), DVE.op(lambda e: e.memset(utail[:], 0.0)),
                 DVE.op(lambda e: e.memset(R[:], 0.0)), DVE.op(lambda e: e.memset(winv[:, 0:1], 0.5)), DVE.op(lambda e: e.memset(winv[:, 1:2], 0.25)),
                 DVE.op(lambda e: e.memset(winv[:, 2:3], 0.125)), DVE.op(lambda e: e.memset(winv[:, 3:4], 0.0625)), DVE.op(lambda e: e.memset(S[:], 0.0)), DVE.op(lambda e: e.memset(rk[:], 1.0))]
        for en in (PE, ACT, DVE):
            en.wait(cev[-1])
            en.wait(c0_ev)
        SP.wait(c0_ev)
        POOL.wait(c0_ev)
        ACT.op(lambda e: e.activation(out=epsb[:], in_=epsb[:], func=AF.Copy))
        PE.op(lambda e: e.matmul(ps[:, 0:128], lhsT=ones[:, :], rhs=ones[:, :], start=True, stop=True))

        xstage = Rf[:, 0:2 * 2048].rearrange("p (a f) -> p a f", a=2)
        tstage = Rf[:, 4096:4096 + 2 * 512].rearrange("p (a j t) -> p a j t", a=2, j=4)
        st["x_free"] = [None, None]
        st["t_free"] = [None, None]
        st["xi"] = 0
        st["ti"] = 0
        h_written = []

        def prepass_block(src_rows, ntok, tile, tcol0):
            for half in range(2):
                xi = st["xi"] % 2
                st["xi"] += 1
                lev = SP.dma(lambda e, o=xstage[0:ntok, xi, :], s=src_rows[:, half * 2048:(half + 1) * 2048]: e.dma_start(out=o, in_=s),
                             xsem[xi], deps=st["x_free"][xi])
                lastT = None
                for g4 in range(4):
                    un = alloc()
                    PE.wait(un.free)
                    PE.wait(lev)
                    for j in range(4):
                        f0 = (g4 * 4 + j) * 128
                        lastT = PE.op(lambda e, o=ps[:, un.u * 512 + j * 128:un.u * 512 + j * 128 + ntok],
                                      i_=xstage[0:ntok, xi, f0:f0 + 128], idn=ident[0:ntok, 0:ntok]:
                                      e.transpose(out=o, in_=i_, identity=idn), sig=(j == 3))
                    ti = st["ti"] % 2
                    st["ti"] += 1
                    cev_ = DVE.op(lambda e, o=tstage[:, ti, :, 0:ntok],
                                  i_=ps[:, un.u * 512:(un.u + 1) * 512].rearrange("p (j t) -> p j t", t=128)[:, :, 0:ntok]:
                                  e.tensor_copy(out=o, in_=i_), deps=[lastT, st["t_free"][ti]])
                    release(un, [cev_])
                    kc0 = half * 16 + g4 * 4
                    sev = SP.dma(lambda e, o=hT[0][tile, :, kc0:kc0 + 4, tcol0:tcol0 + ntok], i_=tstage[:, ti, :, 0:ntok]:
                                 e.dma_start(out=o, in_=i_), tsem[ti], deps=cev_)
                    st["t_free"][ti] = sev
                    h_written.append(sev)
                st["x_free"][xi] = lastT

        prepass_block(meta_in[:, :], NM, 0, 0)
        for g in range(ntiles):
            for tb in range(4):
                prepass_block(xin[g * TS + tb * 128:g * TS + (tb + 1) * 128, :], 128, g + 1, tb * 128)
        st["hT_ready"] = list(h_written[-2:])
        SP.wait(st["hT_ready"])
        hring = {"i": 0, "free": [None] * 4}

        def tile(l, ti, src, dst):
            T = NM if ti == 0 else TS
            g = ti - 1
            col0 = 0 if ti == 0 else NM + TS * g
            nkeys = col0 + T
            nkt = 1 if ti == 0 else 1 + 4 * (g + 1)
            wi = w_in_v[l]

            def rhs_hn(kc):
                return hn[:, kc, 0:T]

            barrier()
            rope_ev = SP.dma(lambda e: e.dma_start(out=ropet[:, :, 0:T], in_=rope_d[:, :, col0:col0 + T]), ropesem,
                             deps=[PE.last, ACT.last, DVE.last])

            un = alloc()
            PE.wait(un.free)
            mev = None
            for kc in range(KC):
                hi = hring["i"] % 4
                hring["i"] += 1
                lev = SP.dma(lambda e, o=hst[:, hi, 0:T], s=src[ti, :, kc, 0:T]: e.dma_start(out=o, in_=s), hsem[hi],
                             deps=[hring["free"][hi]] + st["hT_ready"] + [PE.last, ACT.last, DVE.last])
                sq_i = kc % 2
                sev = ACT.op(lambda e, o=sqs[:, sq_i, 0:T], i_=hst[:, hi, 0:T]: e.activation(out=o, in_=i_, func=AF.Square),
                             deps=[lev, st.get("sq_free%d" % sq_i)])
                hring["free"][hi] = sev
                mev = PE.op(lambda e, o=uap(un, T), r=sqs[:, sq_i, 0:T], s0=(kc == 0), s1=(kc == KC - 1):
                            e.matmul(o, lhsT=ones[:, :], rhs=r, start=s0, stop=s1), deps=sev)
                st["sq_free%d" % sq_i] = mev
            a_ = ACT.op(lambda e: e.activation(out=rbc[:, 0:T], in_=uap(un, T), func=AF.Sqrt, bias=epsb[:, 0:1], scale=1.0 / D), deps=mev)
            rev = DVE.op(lambda e: e.reciprocal(out=rbc[:, 0:T], in_=rbc[:, 0:T]), deps=a_)
            release(un, [a_])
            for kc in range(KC):
                hi = hring["i"] % 4
                hring["i"] += 1
                lev = SP.dma(lambda e, o=hst[:, hi, 0:T], s=src[ti, :, kc, 0:T]: e.dma_start(out=o, in_=s), hsem[hi],
                             deps=hring["free"][hi])
                dev = DVE.op(lambda e, o=hn[:, kc, 0:T], i_=hst[:, hi, 0:T], g_=normg[:, l * 32 + kc:l * 32 + kc + 1]:
                             e.scalar_tensor_tensor(out=o, in0=i_, scalar=g_, in1=rbc[:, 0:T], op0=ALU.mult, op1=ALU.mult),
                             deps=[lev, rev])
                hring["free"][hi] = dev
            barrier()

            if _DBG.get('stop') == 'A':
                return
            def norm_group(colbase, nch, gains, gcol0, dst_fn, ndim):
                ssq = alloc()
                PE.wait(ssq.free)
                raw_evs = []
                mev_ = None
                idx = 0
                for g0 in range(0, nch, 4):
                    res = project(wsimple(wi, colbase + g0 * 128, 512), KC, C4, rhs_hn, T)
                    for j, (u_, dn) in enumerate(res):
                        c = g0 + j
                        sq_i = idx % 2
                        idx += 1
                        a1 = ACT.op(lambda e, o=rawst[:, c, 0:T], i_=uap(u_, T): e.activation(out=o, in_=i_, func=AF.Copy), deps=dn)
                        a2 = ACT.op(lambda e, o=sqs[:, sq_i, 0:T], i_=uap(u_, T): e.activation(out=o, in_=i_, func=AF.Square),
                                    deps=[dn, st.get("sq_free%d" % sq_i)])
                        release(u_, [a2])
                        raw_evs.append(a1)
                        mev_ = PE.op(lambda e, o=uap(ssq, T), r=sqs[:, sq_i, 0:T], s0=(c == 0), s1=(c == nch - 1):
                                     e.matmul(o, lhsT=ones[:, :], rhs=r, start=s0, stop=s1), deps=a2)
                        st["sq_free%d" % sq_i] = mev_
                a_ = ACT.op(lambda e: e.activation(out=rbc[:, 0:T], in_=uap(ssq, T), func=AF.Sqrt, bias=epsb[:, 0:1], scale=1.0 / ndim), deps=[mev_, DVE.last])
                r_ev = DVE.op(lambda e: e.reciprocal(out=rbc[:, 0:T], in_=rbc[:, 0:T]), deps=a_)
                release(ssq, [a_])
                for c in range(nch):
                    DVE.op(lambda e, o=dst_fn(c), i_=rawst[:, c, 0:T], g_=gains[:, gcol0 + c:gcol0 + c + 1]:
                           e.scalar_tensor_tensor(out=o, in0=i_, scalar=g_, in1=rbc[:, 0:T], op0=ALU.mult, op1=ALU.mult),
                           deps=[raw_evs[c], r_ev])
                barrier()

            norm_group(OFF["cq"], 8, qlg, l * 8, lambda c: cq[:, c, 0:T], 1024)
            norm_group(OFF["ckv"], 4, kvg, l * 4, lambda c: cache[:, c, col0:col0 + T], 512)

            def wsrc_kr(kc0, nk):
                b = OFF["kr"]
                return [(0, nk, 64, WS(wi, kc0, nk, b, 64)), (64, nk, 64, WS(wi, kc0, nk, b, 64)),
                        (128, nk, 32, WS(wi, kc0, nk, b + 32, 32)), (160, nk, 32, WS(wi, kc0, nk, b, 32))]
            (ux, dx), (uy, dy) = project(wsrc_kr, KC, [(0, 128), (128, 64)], rhs_hn, T)
            t64 = sB[0:64, 0:2, :]
            a1 = DVE.op(lambda e: e.scalar_tensor_tensor(out=t64[:, 0, 0:T], in0=uap(ux, T, 64), scalar=khg[0:64, l * 4 + 1:l * 4 + 2],
                                                         in1=ropet[:, 0, 0:T], op0=ALU.mult, op1=ALU.mult), deps=[dx, rope_ev])
            a2 = DVE.op(lambda e: e.scalar_tensor_tensor(out=t64[:, 1, 0:T], in0=uap(uy, T, 64), scalar=khg[0:64, l * 4 + 2:l * 4 + 3],
                                                         in1=ropet[:, 1, 0:T], op0=ALU.mult, op1=ALU.mult), deps=[dy, rope_ev])
            a3 = DVE.op(lambda e: e.tensor_tensor(out=cache[0:64, 4, col0:col0 + T], in0=t64[:, 0, 0:T], in1=t64[:, 1, 0:T], op=ALU.add),
                        deps=[a1, a2])
            a4 = ACT.op(lambda e: e.activation(out=cache[64:128, 4, col0:col0 + T], in_=uap(ux, T, 64, 64), func=AF.Square), deps=dx)
            release(ux, [a1, a4])
            release(uy, [a2])
            barrier()

            if _DBG.get('stop') == 'B':
                return
            for gp in range(4):
                w_ = WINS[gp]
                c0 = gp * 4
                res = project(wsimple(wi, OFF["pool"] + gp * 512, 512), KC, C4, rhs_hn, T)
                cevs = []
                for j, (u_, dn) in enumerate(res):
                    a_ = ACT.op(lambda e, o=ue[:, 0, j, NM:NM + T], i_=uap(u_, T): e.activation(out=o, in_=i_, func=AF.Copy), deps=dn)
                    b_ = DVE.op(lambda e, o=uo[:, j, 0:T], i_=ue[:, 0, j, NM:NM + T]: e.tensor_copy(out=o, in_=i_), deps=a_)
                    release(u_, [a_])
                    cevs += [a_, b_]
                if _DBG.get('stop') == 'D1':
                    barrier()
                    return
                if ti == 0:
                    hz = DVE.op(lambda e: e.memset(ue[:, 0, :, 0:NM], 0.0))
                else:
                    hz = DVE.op(lambda e, i_=utail[:, c0:c0 + 4, :]: e.tensor_copy(out=ue[:, 0, :, 0:NM], in_=i_))
                cevs.append(hz)
                src_i = 0
                sh = 1
                last = cevs
                for stp in range(gp + 1):
                    nl_ = []
                    for j in range(4):
                        d_ = DVE.op(lambda e, o=ue[:, 1 - src_i, j, sh:NM + T], a=ue[:, src_i, j, sh:NM + T], b__=ue[:, src_i, j, 0:NM + T - sh]:
                                    e.tensor_tensor(out=o, in0=a, in1=b__, op=ALU.add), deps=last)
                        nl_.append(d_)
                    last = nl_
                    src_i = 1 - src_i
                    sh *= 2
                tl = DVE.op(lambda e, o=utail[:, c0:c0 + 4, :]: e.tensor_copy(out=o, in_=uo[:, :, T - NM:T]), deps=cevs)
                if ti == 0:
                    m_ = DVE.op(lambda e, o=ue[:, src_i, :, NM:NM + T], ic=invc[:, gp, :, :]: e.tensor_tensor(out=o, in0=o, in1=ic, op=ALU.mult),
                                deps=last)
                    f_ = DVE.op(lambda e, a=ue[:, src_i, :, NM:NM + T]: e.tensor_tensor(out=diffT[:, :, 0:T], in0=a, in1=uo[:, :, 0:T], op=ALU.subtract),
                                deps=[m_])
                else:
                    for j in range(4):
                        f_ = DVE.op(lambda e, o=diffT[:, j, 0:T], a=ue[:, src_i, j, NM:NM + T], u0=uo[:, j, 0:T], sc=winv[:, gp:gp + 1]:
                                    e.scalar_tensor_tensor(out=o, in0=a, scalar=sc, in1=u0, op0=ALU.mult, op1=ALU.subtract), deps=last)
                if _DBG.get('stop') == 'D2':
                    barrier()
                    return
                res = project(wsimple(wi, OFF["gb"] + gp * 512, 512), KC, C4, rhs_hn, T)
                sg_evs = []
                for j, (u_, dn) in enumerate(res):
                    a_ = ACT.op(lambda e, o=hst[:, j, 0:T], i_=uap(u_, T): e.activation(out=o, in_=i_, func=AF.Silu), deps=dn)
                    release(u_, [a_])
                    sg_evs.append(a_)
                if _DBG.get('stop') == 'D3':
                    barrier()
                    return
                PE.wait(f_)
                res = project(wsimple(w_pool_v[l, gp], 0, 512), 4, C4, lambda kc: diffT[:, kc, 0:T], T)
                for j, (u_, dn) in enumerate(res):
                    d_ = DVE.op(lambda e, o=Gb[:, c0 + j, 0:T], i_=uap(u_, T), s_=psc[:, l * 16 + c0 + j:l * 16 + c0 + j + 1], g_=hst[:, j, 0:T]:
                                e.scalar_tensor_tensor(out=o, in0=i_, scalar=s_, in1=g_, op0=ALU.mult, op1=ALU.mult), deps=[dn, sg_evs[j]])
                    release(u_, [d_])
                barrier()

            if _DBG.get('stop') == 'D':
                return
            qraw = sB[:, 0:3, :]
            rden = sB[:, 3, :]
            kt_cols = [(0, NM)] + [(NM + 128 * j, 128) for j in range(nkt - 1)]

            def prep(h):
                bi = h % 2
                def wsrc_q(kc0, nk):
                    b = h * 192
                    wq = w_uq_v[l]
                    return [(0, nk, 192, WS(wq, kc0, nk, b, 192)), (192, nk, 32, WS(wq, kc0, nk, b + 160, 32)),
                            (224, nk, 32, WS(wq, kc0, nk, b + 128, 32))]
                (un_, dn_), (ur_, dr_), (us_, ds_) = project(wsrc_q, 8, [(0, 128), (128, 64), (192, 64)], lambda kc: cq[:, kc, 0:T], T)
                ssq = alloc()
                PE.wait(ssq.free)
                c1 = ACT.op(lambda e: e.activation(out=qraw[:, 0, 0:T], in_=uap(un_, T), func=AF.Copy), deps=[dn_, st.get("qraw_free")])
                s1 = ACT.op(lambda e: e.activation(out=sqs[:, 0, 0:T], in_=uap(un_, T), func=AF.Square), deps=[dn_, st.get("sq_free0")])
                c2 = ACT.op(lambda e: e.activation(out=qraw[0:64, 1, 0:T], in_=uap(ur_, T, 64), func=AF.Copy), deps=dr_)
                s2 = ACT.op(lambda e: e.activation(out=sqs[0:64, 1, 0:T], in_=uap(ur_, T, 64), func=AF.Square), deps=[dr_, st.get("sq_free1")])
                c3 = ACT.op(lambda e: e.activation(out=qraw[0:64, 2, 0:T], in_=uap(us_, T, 64), func=AF.Copy), deps=ds_)
                release(un_, [s1])
                release(ur_, [s2])
                release(us_, [c3])
                PE.op(lambda e: e.matmul(uap(ssq, T), lhsT=ones[:, :], rhs=sqs[:, 0, 0:T], start=True, stop=False), sig=False, deps=s1)
                m_ = PE.op(lambda e: e.matmul(uap(ssq, T), lhsT=ones[0:64, :], rhs=sqs[0:64, 1, 0:T], start=False, stop=True), deps=s2)
                st["sq_free0"] = m_
                st["sq_free1"] = m_
                a_ = ACT.op(lambda e: e.activation(out=rbc[:, 0:T], in_=uap(ssq, T), func=AF.Sqrt, bias=epsb[:, 0:1], scale=1.0 / 192), deps=[m_, DVE.last])
                r_ = DVE.op(lambda e: e.reciprocal(out=rbc[:, 0:T], in_=rbc[:, 0:T]), deps=a_)
                release(ssq, [a_])
                qf = st.get("Q_free%d" % bi)
                q1 = DVE.op(lambda e: e.scalar_tensor_tensor(out=Qn[:, bi, 0:T], in0=qraw[:, 0, 0:T], scalar=qhg[:, l * 4:l * 4 + 1], in1=rbc[:, 0:T],
                                                             op0=ALU.mult, op1=ALU.mult), deps=[c1, r_, qf])
                t1 = DVE.op(lambda e: e.scalar_tensor_tensor(out=qraw[0:64, 1, 0:T], in0=qraw[0:64, 1, 0:T], scalar=qhg[0:64, l * 4 + 1:l * 4 + 2],
                                                             in1=ropet[:, 0, 0:T], op0=ALU.mult, op1=ALU.mult), deps=[c2, rope_ev])
                t2 = DVE.op(lambda e: e.scalar_tensor_tensor(out=qraw[0:64, 2, 0:T], in0=qraw[0:64, 2, 0:T], scalar=qhg[0:64, l * 4 + 2:l * 4 + 3],
                                                             in1=ropet[:, 1, 0:T], op0=ALU.mult, op1=ALU.mult), deps=[c3, t1])
                t3 = DVE.op(lambda e: e.tensor_tensor(out=qraw[0:64, 1, 0:T], in0=qraw[0:64, 1, 0:T], in1=qraw[0:64, 2, 0:T], op=ALU.add), deps=t2)
                q2 = DVE.op(lambda e: e.tensor_tensor(out=Qr[:, bi, 0:T], in0=qraw[0:64, 1, 0:T], in1=rbc[0:64, 0:T], op=ALU.mult), deps=t3)
                st["qraw_free"] = q2
                i, wev = wload([(0, 4, 256, WS(w_ukv_v[l], 0, 4, h * 256, 256))])
                PE.wait(wev)
                kvf = st.get("KV_free%d" % bi)
                PE.wait(kvf)
                rku = alloc()
                PE.wait(rku.free)
                k_evs = []
                lastmm = None
                seg_starts = [0] + list(range(400, nkeys, 512))
                for sidx, s0 in enumerate(seg_starts):
                    n = min(400 if s0 == 0 else 512, nkeys - s0)
                    un2 = alloc()
                    PE.wait(un2.free)
                    for kc in range(4):
                        lastmm = PE.op(lambda e, o=uap(un2, n), l_=wbuf[:, i, kc, 0:128], r=cache[:, kc, s0:s0 + n], a=(kc == 0), b=(kc == 3):
                                       e.matmul(o, lhsT=l_, rhs=r, start=a, stop=b), sig=(kc == 3))
                    d_ = DVE.op(lambda e, o=Kh[bi][:, s0:s0 + n], i_=uap(un2, n): e.tensor_scalar(out=o, in0=i_, scalar1=khg[:, l * 4:l * 4 + 1], scalar2=0.0, op0=ALU.mult, op1=ALU.add),
                                deps=[lastmm, kvf])
                    sqi = sidx % 2
                    a_ = ACT.op(lambda e, o=sqs[:, sqi, 0:n], i_=uap(un2, n): e.activation(out=o, in_=i_, func=AF.Square),
                                deps=[lastmm, d_, st.get("sq_free%d" % sqi)])
                    release(un2, [d_, a_])
                    k_evs.append(d_)
                    mm_ = None
                    for kt, (kc0_, nk_) in enumerate(kt_cols):
                        if kc0_ < s0 or kc0_ >= s0 + n:
                            continue
                        PE.op(lambda e, o=ps[0:nk_, rku.u * 512 + kt:rku.u * 512 + kt + 1], l_=sqs[:, sqi, kc0_ - s0:kc0_ - s0 + nk_]:
                              e.matmul(o, lhsT=l_, rhs=ones[:, 0:1], start=True, stop=False), sig=False, deps=a_)
                        mm_ = PE.op(lambda e, o=ps[0:nk_, rku.u * 512 + kt:rku.u * 512 + kt + 1], l_=cache[64:128, 4, kc0_:kc0_ + nk_]:
                                    e.matmul(o, lhsT=l_, rhs=ones[64:128, 0:1], start=False, stop=True), sig=False)
                    lastmm = PE.op(lambda e: e.matmul(ps[0:1, rku.u * 512 + 64:rku.u * 512 + 65], lhsT=ones[0:1, 0:1], rhs=ones[0:1, 0:1], start=True, stop=True))
                    st["sq_free%d" % sqi] = lastmm
                a_ = ACT.op(lambda e: e.activation(out=rk[0:NM, bi, 0:1], in_=ps[0:NM, rku.u * 512:rku.u * 512 + 1], func=AF.Sqrt, bias=epsk[0:NM, 0:1], scale=1.0),
                            deps=[lastmm, kvf])
                if nkt > 1:
                    a_ = ACT.op(lambda e: e.activation(out=rk[:, bi, 1:nkt], in_=ps[:, rku.u * 512 + 1:rku.u * 512 + nkt], func=AF.Sqrt, bias=epsk[:, 0:1], scale=1.0),
                                deps=[lastmm, kvf])
                r2 = DVE.op(lambda e: e.reciprocal(out=rk[:, bi, 0:nkt], in_=rk[:, bi, 0:nkt]), deps=a_)
                release(rku, [a_])
                v_evs = []
                for k0 in range(0, nkt, 4):
                    nn = min(4, nkt - k0)
                    un2 = alloc()
                    PE.wait(un2.free)
                    for jj in range(nn):
                        kc0_, nk_ = kt_cols[k0 + jj]
                        for kc in range(4):
                            lastmm = PE.op(lambda e, o=ps[0:nk_, un2.u * 512 + jj * 128:un2.u * 512 + jj * 128 + 128], l_=cache[:, kc, kc0_:kc0_ + nk_],
                                           r=wbuf[:, i, kc, 128:256], a=(kc == 0), b=(kc == 3): e.matmul(o, lhsT=l_, rhs=r, start=a, stop=b),
                                           sig=(kc == 3 and jj == nn - 1))
                    if k0 == 0:
                        pass
                    if k0 == 0:
                        c0_ = ACT.op(lambda e, o=Vh[bi][0:NM, 0, :], i_=ps[0:NM, un2.u * 512:un2.u * 512 + 128]:
                                     e.activation(out=o, in_=i_, func=AF.Copy), deps=[lastmm, kvf])
                        c_ = c0_
                        if nn > 1:
                            c_ = ACT.op(lambda e, o=Vh[bi][:, 1:nn, :], i_=ps[:, un2.u * 512 + 128:un2.u * 512 + nn * 128].rearrange("p (k d) -> p k d", d=128):
                                        e.activation(out=o, in_=i_, func=AF.Copy), deps=[lastmm, kvf])
                    else:
                        c_ = ACT.op(lambda e, o=Vh[bi][:, k0:k0 + nn, :], i_=ps[:, un2.u * 512:un2.u * 512 + nn * 128].rearrange("p (k d) -> p k d", d=128):
                                    e.activation(out=o, in_=i_, func=AF.Copy), deps=[lastmm, kvf])
                    release(un2, [c_])
                    v_evs.append(c_)
                wstate["free"][i] = lastmm
                return dict(q=[q1, q2], k=k_evs, v=v_evs, rk=r2)

            def attend(h, pe_):
                bi = h % 2
                PE.wait(pe_["q"])
                PE.wait(pe_["k"])
                PE.wait(pe_["v"])
                ou = alloc()
                du = alloc()
                PE.wait(ou.free)
                PE.wait(du.free)
                last_pv = None
                last_exp = None
                pend = None

                def emit_pv(kt, nk_, qc0, own, pi, ex):
                    PE.wait(ex)
                    ranges = [(qc0, T, nk_)]
                    if own:
                        ranges = [(qc0 + 64, T, nk_), (qc0, qc0 + 64, 64)]
                    ranges = [rg for rg in ranges if rg[1] > rg[0]]
                    lp = None
                    for ri, (a, b, kk) in enumerate(ranges):
                        fin_ = (kt == nkt - 1) and (ri == len(ranges) - 1)
                        PE.op(lambda e, o=ps[:, ou.u * 512 + a:ou.u * 512 + b], l_=Vh[bi][0:kk, kt, :], r=pbuf[0:kk, pi, a:b], s0=(kt == 0), s1=fin_:
                              e.matmul(o, lhsT=l_, rhs=r, start=s0, stop=s1), sig=False)
                        lp = PE.op(lambda e, o=ps[:, du.u * 512 + a:du.u * 512 + b], l_=ones[0:kk, :], r=pbuf[0:kk, pi, a:b], s0=(kt == 0), s1=fin_:
                                   e.matmul(o, lhsT=l_, rhs=r, start=s0, stop=s1))
                    st["p_free%d" % pi] = lp
                    return lp

                for kt, (kc0_, nk_) in enumerate(kt_cols):
                    own = (ti > 0) and (kt >= 1 + 4 * g)
                    jj = (kt - 1 - 4 * g) if own else 0
                    qc0 = 128 * jj if own else 0
                    su = alloc()
                    PE.wait(su.free)
                    PE.op(lambda e, o=ps[0:nk_, su.u * 512 + qc0:su.u * 512 + T], l_=Kh[bi][:, kc0_:kc0_ + nk_], r=Qn[:, bi, qc0:T]:
                          e.matmul(o, lhsT=l_, rhs=r, start=True, stop=False), sig=False)
                    sm = PE.op(lambda e, o=ps[0:nk_, su.u * 512 + qc0:su.u * 512 + T], l_=cache[0:64, 4, kc0_:kc0_ + nk_], r=Qr[:, bi, qc0:T]:
                               e.matmul(o, lhsT=l_, rhs=r, start=False, stop=True))
                    pi = kt % 3
                    ex = ACT.op(lambda e, o=pbuf[0:nk_, pi, qc0:T], i_=ps[0:nk_, su.u * 512 + qc0:su.u * 512 + T], sc=rk[0:nk_, bi, kt:kt + 1]:
                                e.activation(out=o, in_=i_, func=AF.Exp, scale=sc), deps=[sm, pe_["rk"], st.get("p_free%d" % pi)])
                    release(su, [ex])
                    last_exp = ex
                    if pend is not None:
                        last_pv = emit_pv(*pend)
                    pend = (kt, nk_, qc0, own, pi, ex)
                last_pv = emit_pv(*pend)
                st["KV_free%d" % bi] = last_pv
                st["Q_free%d" % bi] = last_pv
                r_ = DVE.op(lambda e: e.reciprocal(out=rden[:, 0:T], in_=uap(du, T)), deps=last_pv)
                o1 = DVE.op(lambda e: e.tensor_tensor(out=rden[:, 0:T], in0=uap(ou, T), in1=rden[:, 0:T], op=ALU.mult), deps=r_)
                o2 = DVE.op(lambda e: e.tensor_tensor(out=Ga[:, h, 0:T], in0=rden[:, 0:T], in1=hst[:, h % 4, 0:T], op=ALU.mult),
                            deps=[o1, st["sg_ev"][h % 4]])
                release(ou, [o1])
                release(du, [r_])
                return o2

            for h4 in range(4):
                res = project(wsimple(wi, OFF["ga"] + h4 * 512, 512), KC, C4, rhs_hn, T)
                st["sg_ev"] = []
                for j, (u_, dn) in enumerate(res):
                    a_ = ACT.op(lambda e, o=hst[:, j, 0:T], i_=uap(u_, T): e.activation(out=o, in_=i_, func=AF.Silu), deps=[dn, DVE.last])
                    release(u_, [a_])
                    st["sg_ev"].append(a_)
                for hh in range(4):
                    h = h4 * 4 + hh
                    pe_ = prep(h)
                    attend(h, pe_)
                barrier()

            if _DBG.get('stop') == 'C':
                return
            for j4 in range(8):
                res = project(wsimple(wi, OFF["ma"] + j4 * 512, 512), KC, C4, rhs_hn, T)
                sa_ev = []
                for j, (u_, dn) in enumerate(res):
                    a_ = ACT.op(lambda e, o=hst[:, j, 0:T], i_=uap(u_, T): e.activation(out=o, in_=i_, func=AF.Sigmoid), deps=[dn, DVE.last])
                    release(u_, [a_])
                    sa_ev.append(a_)
                res = project(wsimple(wi, OFF["mb"] + j4 * 512, 512), KC, C4, rhs_hn, T)
                sb_ev = []
                for j, (u_, dn) in enumerate(res):
                    a_ = ACT.op(lambda e, o=sB[:, j, 0:T], i_=uap(u_, T): e.activation(out=o, in_=i_, func=AF.Sigmoid), deps=[dn, DVE.last])
                    release(u_, [a_])
                    sb_ev.append(a_)
                res = project(wsimple(w_ba_v[l], j4 * 512, 512), 16, C4, lambda kc: Ga[:, kc, 0:T], T)
                ta_ev = []
                for j, (u_, dn) in enumerate(res):
                    d_ = DVE.op(lambda e, o=hst[:, j, 0:T], i_=uap(u_, T): e.tensor_tensor(out=o, in0=i_, in1=o, op=ALU.mult), deps=[dn, sa_ev[j]])
                    release(u_, [d_])
                    ta_ev.append(d_)
                res = project(wsimple(w_bb_v[l], j4 * 512, 512), 16, C4, lambda kc: Gb[:, kc, 0:T], T)
                for j, (u_, dn) in enumerate(res):
                    d_ = DVE.op(lambda e, o=sB[:, j, 0:T], i_=uap(u_, T): e.tensor_tensor(out=o, in0=i_, in1=o, op=ALU.mult), deps=[dn, sb_ev[j]])
                    release(u_, [d_])
                    DVE.op(lambda e, o=merged[:, j4 * 4 + j, 0:T], a=hst[:, j, 0:T], b=sB[:, j, 0:T]: e.tensor_tensor(out=o, in0=a, in1=b, op=ALU.add),
                           deps=[d_, ta_ev[j]])
                barrier()

            if _DBG.get('stop') == 'F':
                return
            for o4 in range(8):
                hl = []
                for j in range(4):
                    oc = o4 * 4 + j
                    lev = SP.dma(lambda e, o=hst[:, j, 0:T], s=src[ti, :, oc, 0:T]: e.dma_start(out=o, in_=s), hsem[j],
                                 deps=[st.get("ho_ev%d" % j), DVE.last, ACT.last])
                    hl.append(lev)
                res = project(wsimple(w_out_v[l], o4 * 512, 512), KC, C4, lambda kc: merged[:, kc, 0:T], T)
                for j, (u_, dn) in enumerate(res):
                    oc = o4 * 4 + j
                    d_ = DVE.op(lambda e, o=hst[:, j, 0:T], i_=uap(u_, T): e.tensor_tensor(out=o, in0=i_, in1=o, op=ALU.add), deps=[dn, hl[j]])
                    release(u_, [d_])
                    sev = SP.dma(lambda e, o=dst[ti, :, oc, 0:T], i_=hst[:, j, 0:T]: e.dma_start(out=o, in_=i_), hosem[j], deps=d_)
                    st["ho_ev%d" % j] = sev
                    st["last_store"] = sev
                barrier()
            for j in range(4):
                hring["free"][j] = st.get("ho_ev%d" % j)

        cur = 0
        for li, l in enumerate(layers):
            epoch("L%d" % li)
            if li > 0:
                for i_ in range(NW):
                    wsem[i_] = dsem("w%d_%d" % (i_, li))
                for i_ in range(4):
                    hsem[i_] = dsem("h%d_%d" % (i_, li))
                    hosem[i_] = dsem("ho%d_%d" % (i_, li))
            for ti in range(ntiles + 1):
                tile(l, ti, hT[cur], hT[1 - cur])
            st["hT_ready"] = [st.get("ho_ev%d" % j) for j in range(4)]
            cur = 1 - cur

        barrier()
        SP.wait(st["hT_ready"])
        ostage = Rf[:, 0:2 * 4096].rearrange("p (a f) -> p a f", a=2)
        fst = S[:, 0:2 * 4 * 128].rearrange("p (a j t) -> p a j t", a=2, j=4)
        osem = [dsem("os%d" % i) for i in range(2)]
        fsem = [dsem("fs%d" % i) for i in range(2)]
        st["o_free"] = [None, None]
        st["f_free"] = [None, None]
        fin_evs = []
        blk = 0
        fi = 0
        for ti in range(ntiles + 1):
            ntok_all = NM if ti == 0 else TS
            for tb in range(0, ntok_all, 128):
                ntok = min(128, ntok_all - tb)
                oi = blk % 2
                blk += 1
                cp = None
                for g4 in range(8):
                    fb = fi % 2
                    fi += 1
                    lev = SP.dma(lambda e, o=fst[:, fb, :, 0:ntok], s=hT[cur][ti, :, g4 * 4:g4 * 4 + 4, tb:tb + ntok]: e.dma_start(out=o, in_=s),
                                 fsem[fb], deps=[st["f_free"][fb], PE.last, DVE.last, ACT.last] if g4 < 2 and blk == 1 else st["f_free"][fb])
                    un = alloc()
                    PE.wait(un.free)
                    PE.wait(lev)
                    lt = None
                    for j in range(4):
                        lt = PE.op(lambda e, o=ps[0:ntok, un.u * 512 + j * 128:un.u * 512 + (j + 1) * 128], i_=fst[:, fb, j, 0:ntok]:
                                   e.transpose(out=o, in_=i_, identity=ident[:, :]), sig=(j == 3))
                    st["f_free"][fb] = lt
                    cp = DVE.op(lambda e, o=ostage[0:ntok, oi, g4 * 512:(g4 + 1) * 512], i_=ps[0:ntok, un.u * 512:(un.u + 1) * 512]:
                                e.tensor_copy(out=o, in_=i_), deps=[lt, st["o_free"][oi]])
                    release(un, [cp])
                row0 = NT * TS if ti == 0 else (ti - 1) * TS + tb
                sev = SP.dma(lambda e, o=out_d[row0:row0 + ntok, :], i_=ostage[0:ntok, oi, :]: e.dma_start(out=o, in_=i_), osem[oi], deps=cp)
                st["o_free"][oi] = sev
                fin_evs.append(sev)
        SP.wait(fin_evs[-2:])

        _DBG['engs'] = (PE, ACT, DVE, POOL, SP)
        if _DBG.get('nosim') is None:
            _deadlock_check((PE, ACT, DVE, POOL, SP))
        with nc.Block() as block:
            @block.tensor
            def _(e):
                PE.replay(e)

            @block.scalar
            def _(e):
                ACT.replay(e)

            @block.vector
            def _(e):
                DVE.replay(e)

            @block.gpsimd
            def _(e):
                POOL.replay(e)

            @block.sync
            def _(e):
                SP.replay(e)
    return nc


def _host_consts():
    pos = np.arange(KEYS, dtype=np.float32)
    inv = (1.0 / (np.float32(10000.0) ** (np.arange(0, 64, 2, dtype=np.float32) / np.float32(64)))).astype(np.float32)
    ang = (pos[:, None] * inv[None, :]).astype(np.float32)
    cos = np.cos(ang).astype(np.float32).T
    sin = np.sin(ang).astype(np.float32).T
    rope = np.zeros((64, 2, KEYS), np.float32)
    rope[0:32, 0] = cos
    rope[32:64, 0] = cos
    rope[0:32, 1] = -sin
    rope[32:64, 1] = sin
    invc = np.zeros((128, 4, 4, NM), np.float32)
    p = np.arange(NM, dtype=np.float32)
    for gi, w in enumerate(WINS):
        invc[:, gi, :, :] = (1.0 / np.minimum(p + 1.0, np.float32(w))).astype(np.float32)[None, None, :]
    return rope, invc.reshape(128, -1), np.eye(128, dtype=np.float32)


def _pp(v, nch):
    return np.ascontiguousarray(v.reshape(DEPTH, nch, 128).transpose(2, 0, 1).reshape(128, DEPTH * nch))


def _head_gain(g):
    o = np.zeros((128, DEPTH, 4), np.float32)
    for l in range(DEPTH):
        o[:, l, 0] = g[l, 0:128]
        o[0:64, l, 1] = g[l, 128:192]
        o[0:32, l, 2] = g[l, 160:192]
        o[32:64, l, 2] = g[l, 128:160]
    return np.ascontiguousarray(o.reshape(128, DEPTH * 4))


_NC_CACHE = {}


def _launch(layers, x_cores, meta_cores, weights, consts):
    key = tuple(layers)
    if key not in _NC_CACHE:
        _NC_CACHE[key] = build_program(list(layers))
    nc = _NC_CACHE[key]
    in_maps = []
    for c in range(8):
        m = dict(weights)
        m.update(consts)
        m["xin"] = x_cores[c % 4]
        m["meta"] = meta_cores[c % 4]
        in_maps.append(m)
    res = run_bass_kernel_spmd(nc, in_maps, core_ids=list(range(8)))
    return [res.results[c]["out"] for c in range(4)]


def kernel(x, meta_tokens, norm_g, w_in, q_lora_g, kv_lora_g, w_uq, w_ukv, q_head_g, k_head_g,
           w_pool, pool_scale, w_branch_a, w_branch_b, w_out):
    f = lambda a: np.ascontiguousarray(np.asarray(a, dtype=np.float32))
    x = f(x)
    rope, invc, ident = _host_consts()
    weights = dict(w_in=f(w_in), w_uq=f(w_uq), w_ukv=f(w_ukv), w_pool=f(w_pool), w_branch_a=f(w_branch_a),
                   w_branch_b=f(w_branch_b), w_out=f(w_out))
    consts = dict(normg=_pp(f(norm_g), 32), qlg=_pp(f(q_lora_g), 8), kvg=_pp(f(kv_lora_g), 4),
                  qhg=_head_gain(f(q_head_g)), khg=_head_gain(f(k_head_g)), pscale=_pp(f(pool_scale), 16),
                  ropet=rope, invc=invc, ident=ident)
    xc = [x[b] for b in range(4)]
    mc = [f(meta_tokens) for _ in range(4)]
    outs = _launch(tuple(range(DEPTH)), xc, mc, weights, consts)
    return np.stack([o[0:NT * TS] for o in outs], axis=0)
```
